# Optimizing a Trainium2 kernel written in Bass

```python
import math
import jax, jax.numpy as jnp
from jax import lax
import numpy as np

D_MODEL = 1024
BATCH = 8
SEQ = 4096
DEPTH = 2

N_A_LAYERS = DEPTH // 2
N_B_LAYERS = DEPTH - N_A_LAYERS

A_HEADS = 16
A_HEAD_DIM = D_MODEL // A_HEADS
A_QKV_WIDTH = 3 * A_HEADS * A_HEAD_DIM
Q_BLOCK = 128

B_GROUPS = ((128, 1), (512, 4), (2048, 16))
B_N_GROUPS = len(B_GROUPS)
B_HEADS_PER_GROUP = 8
B_HEAD_DIM = 64
B_WINDOW_STEPS = 128
B_Q_WIDTH = B_N_GROUPS * B_HEADS_PER_GROUP * B_HEAD_DIM
B_OUT_WIDTH = B_HEADS_PER_GROUP * B_HEAD_DIM
B_KV_WIDTH = 2 * B_Q_WIDTH
ALIBI_TOTAL_HEADS = B_N_GROUPS * B_HEADS_PER_GROUP

D_FF = 2816
CONV_WIDTH = 3

RMS_EPS = 1e-6

kernel_name = "yoco_fox_dilated_convffn_hybrid"


def rms_norm(x, g):
    xf = x.astype(jnp.float32)
    y = xf * lax.rsqrt(jnp.mean(xf * xf, axis=-1, keepdims=True) + RMS_EPS)
    return (y * g.astype(jnp.float32)).astype(x.dtype)


def conv_ffn(h, w_up, conv_w, conv_b, w_down):
    u = h @ w_up
    c = u.shape[-1]
    u = lax.conv_general_dilated(
        u, conv_w[:, None, :].astype(u.dtype), window_strides=(1,),
        padding=[(CONV_WIDTH - 1, 0)],
        dimension_numbers=("NWC", "WIO", "NWC"),
        feature_group_count=c) + conv_b
    a, gate = jnp.split(u, 2, axis=-1)
    return (jax.nn.silu(gate) * a) @ w_down


def forgetting_attention(q, k, v, log_f):
    S = q.shape[1]
    dh = q.shape[-1]
    scale = dh ** -0.5
    c = jnp.cumsum(log_f.astype(jnp.float32), axis=1).transpose(0, 2, 1)
    outs = []
    for i in range(S // Q_BLOCK):
        q0 = i * Q_BLOCK
        q1 = q0 + Q_BLOCK
        s = jnp.einsum("bqhd,bkhd->bhqk", q[:, q0:q1], k[:, :q1]).astype(jnp.float32) * scale
        s = s + c[:, :, q0:q1, None] - c[:, :, None, :q1]
        causal = jnp.arange(q0, q1)[:, None] >= jnp.arange(q1)[None, :]
        p = jax.nn.softmax(jnp.where(causal, s, -jnp.inf), axis=-1)
        outs.append(jnp.einsum("bhqk,bkhd->bqhd", p.astype(v.dtype), v[:, :q1]))
    return jnp.concatenate(outs, axis=1)


def dilated_branch(q, k, v, dil, slopes):
    B, S, H, Dh = q.shape
    W = B_WINDOW_STEPS
    period = dil * W
    Sp = -(-S // period) * period
    nb = Sp // period
    pad = ((0, 0), (0, Sp - S), (0, 0), (0, 0))

    def blocks(t):
        return jnp.pad(t, pad).reshape(B, nb, W, dil, H, Dh)

    def with_prev(t):
        prev = jnp.pad(t, ((0, 0), (1, 0), (0, 0), (0, 0), (0, 0), (0, 0)))[:, :nb]
        return jnp.concatenate([prev, t], axis=2)

    qb = blocks(q)
    kk = with_prev(blocks(k))
    vv = with_prev(blocks(v))
    s = jnp.einsum("bnirhd,bnjrhd->bnrhij", qb, kk).astype(jnp.float32) * (Dh ** -0.5)
    qi = jnp.arange(W)[:, None]
    kj = jnp.arange(2 * W)[None, :]
    dist = qi + W - kj
    band = (dist >= 0) & (dist <= W)
    first = (jnp.arange(nb)[:, None, None] == 0) & (kj < W)[None]
    valid = band[None] & ~first
    bias = -slopes[:, None, None] * (dil * dist).astype(jnp.float32)
    s = jnp.where(valid[None, :, None, None], s + bias, -jnp.inf)
    m = jnp.max(s, axis=-1, keepdims=True)
    p = jnp.exp(s - m)
    l = jnp.sum(p, axis=-1, keepdims=True)
    o = jnp.einsum("bnrhij,bnjrhd->bnirhd", (p / l).astype(v.dtype), vv)
    lse = (m + jnp.log(l))[..., 0]
    o = o.reshape(B, Sp, H, Dh)[:, :S]
    lse = lse.transpose(0, 1, 4, 2, 3).reshape(B, Sp, H)[:, :S]
    return o, lse


def setup_inputs(seed: int = 0) -> dict:
    key = jax.random.key(seed)
    ks = jax.random.split(key, 16)
    f32 = jnp.float32

    def nrm(k, shape, fan_in):
        return jax.random.normal(k, shape, f32) * (fan_in ** -0.5)

    def gain(k, shape):
        return 1.0 + 0.05 * jax.random.normal(k, shape, f32)

    return {
        "x": jax.random.normal(ks[0], (BATCH, SEQ, D_MODEL), f32),
        "a_w_in": nrm(ks[1], (N_A_LAYERS, D_MODEL, A_QKV_WIDTH + A_HEADS), D_MODEL),
        "a_b_f": 2.0 + 0.1 * jax.random.normal(ks[2], (N_A_LAYERS, A_HEADS), f32),
        "a_w_out": nrm(ks[3], (N_A_LAYERS, A_HEADS * A_HEAD_DIM, D_MODEL), A_HEADS * A_HEAD_DIM),
        "b_w_q": nrm(ks[4], (N_B_LAYERS, D_MODEL, B_Q_WIDTH), D_MODEL),
        "b_w_out": nrm(ks[5], (N_B_LAYERS, B_OUT_WIDTH, D_MODEL), B_OUT_WIDTH),
        "kv_norm_g": gain(ks[6], (D_MODEL,)),
        "w_kv": nrm(ks[7], (D_MODEL, B_KV_WIDTH), D_MODEL),
        "mix_norm_g": gain(ks[8], (DEPTH, D_MODEL)),
        "ffn_norm_g": gain(ks[9], (DEPTH, D_MODEL)),
        "ffn_w_up": nrm(ks[10], (DEPTH, D_MODEL, 2 * D_FF), D_MODEL),
        "ffn_conv_w": nrm(ks[11], (DEPTH, CONV_WIDTH, 2 * D_FF), CONV_WIDTH),
        "ffn_conv_b": 0.02 * jax.random.normal(ks[12], (DEPTH, 2 * D_FF), f32),
        "ffn_w_down": nrm(ks[13], (DEPTH, D_FF, D_MODEL), D_FF),
        "final_norm_g": gain(ks[14], (D_MODEL,)),
    }


def reference(x, a_w_in, a_b_f, a_w_out, b_w_q, b_w_out, kv_norm_g, w_kv,
              mix_norm_g, ffn_norm_g, ffn_w_up, ffn_conv_w, ffn_conv_b, ffn_w_down,
              final_norm_g):
    B, S, D = x.shape
    slopes = jnp.exp2(-8.0 * jnp.arange(1, ALIBI_TOTAL_HEADS + 1, dtype=jnp.float32)
                      / ALIBI_TOTAL_HEADS).reshape(B_N_GROUPS, B_HEADS_PER_GROUP)
    kv = None
    for layer in range(DEPTH):
        if layer < N_A_LAYERS:
            h = rms_norm(x, mix_norm_g[layer])
            proj = h @ a_w_in[layer]
            qkv = proj[..., :A_QKV_WIDTH].reshape(B, S, 3, A_HEADS, A_HEAD_DIM)
            log_f = jax.nn.log_sigmoid(
                proj[..., A_QKV_WIDTH:].astype(jnp.float32) + a_b_f[layer].astype(jnp.float32))
            o = forgetting_attention(qkv[:, :, 0], qkv[:, :, 1], qkv[:, :, 2], log_f)
            x = x + o.reshape(B, S, A_HEADS * A_HEAD_DIM) @ a_w_out[layer]
        else:
            if kv is None:
                kv = (rms_norm(x, kv_norm_g) @ w_kv).reshape(
                    B, S, 2, B_N_GROUPS, B_HEADS_PER_GROUP, B_HEAD_DIM)
            bl = layer - N_A_LAYERS
            h = rms_norm(x, mix_norm_g[layer])
            q = (h @ b_w_q[bl]).reshape(B, S, B_N_GROUPS, B_HEADS_PER_GROUP, B_HEAD_DIM)
            outs, lses = [], []
            for g, (window, dil) in enumerate(B_GROUPS):
                o_g, lse_g = dilated_branch(q[:, :, g], kv[:, :, 0, g], kv[:, :, 1, g],
                                            dil, slopes[g])
                outs.append(o_g)
                lses.append(lse_g)
            alpha = jax.nn.softmax(jnp.stack(lses, axis=0), axis=0)
            o = jnp.sum(alpha[..., None].astype(outs[0].dtype) * jnp.stack(outs, axis=0), axis=0)
            x = x + o.reshape(B, S, B_OUT_WIDTH) @ b_w_out[bl]
        h = rms_norm(x, ffn_norm_g[layer])
        x = x + conv_ffn(h, ffn_w_up[layer], ffn_conv_w[layer], ffn_conv_b[layer],
                         ffn_w_down[layer])
    return rms_norm(x, final_norm_g)
```

```python
import contextlib
import numpy as np
import ml_dtypes
import concourse.bass as bass
import concourse.mybir as mybir
from concourse.bass_utils import run_bass_kernel_spmd

F32 = mybir.dt.float32
BF16 = mybir.dt.bfloat16
AF = mybir.ActivationFunctionType
ALU = mybir.AluOpType

D = 1024
NH_A = 16
DFF = 2816
NCH = 44
NJ = 22
EPS = 1e-6
B_GROUPS = ((128, 1), (512, 4), (2048, 16))


class T:
    __slots__ = ("sem", "val")

    def __init__(self, sem, val):
        self.sem = sem
        self.val = val


class Eng:
    def __init__(self, name, b, sem):
        self.name = name
        self.b = b
        self.sem = sem
        self.count = 0
        self.waited = {}

    def wait(self, tickets):
        for t in tickets:
            sem, val = t.sem, t.val
            if val <= 0:
                continue
            if self.waited.get(sem, 0) < val:
                self.b.wait_ge(sem, val)
                self.waited[sem] = val


class FW:
    def __init__(self, nc, sems):
        self.nc = nc
        self.sems = list(sems)
        self.pe = Eng("pe", nc.tensor, self.sems.pop())
        self.act = Eng("act", nc.scalar, self.sems.pop())
        self.dve = Eng("dve", nc.vector, self.sems.pop())
        self.pool = Eng("pool", nc.gpsimd, self.sems.pop())
        self.sp = Eng("sp", nc.sync, None)
        self.engs = [self.pe, self.act, self.dve, self.pool, self.sp]
        self.lastw = {}
        self.readers = {}
        self.dmasem = {}

    def _deps(self, reads, writes):
        t = []
        for k in reads:
            if k in self.lastw:
                t.append(self.lastw[k])
        for k in writes:
            if k in self.lastw:
                t.append(self.lastw[k])
            t += list(self.readers.get(k, {}).values())
        return t

    def _commit(self, ticket, reads, writes):
        for k in writes:
            self.lastw[k] = ticket
            self.readers[k] = {}
        for k in reads:
            d = self.readers.setdefault(k, {})
            if ticket.sem not in d or d[ticket.sem].val < ticket.val:
                d[ticket.sem] = ticket

    def op(self, eng, fn, reads=(), writes=()):
        eng.wait(self._deps(reads, writes))
        ins = fn(eng.b)
        eng.count += 1
        ins.then_inc(eng.sem, 1)
        self._commit(T(eng.sem, eng.count), reads, writes)

    def mm(self, fn, reads=(), writes=(), last=False):
        eng = self.pe
        deps = [t for t in self._deps(reads, writes) if t.sem is not eng.sem]
        eng.wait(deps)
        ins = fn(eng.b)
        self._commit(T(eng.sem, eng.count + 1), reads, writes)
        if last:
            eng.count += 1
            ins.then_inc(eng.sem, 1)

    def dma(self, eng, semname, out, in_, reads=(), writes=(), **kw):
        if semname not in self.dmasem:
            self.dmasem[semname] = [self.sems.pop(), 0, []]
        ent = self.dmasem[semname]
        eng.wait(self._deps(reads, writes))
        ins = eng.b.dma_start(out=out, in_=in_, **kw)
        if eng.waited.get(ent[0], 0) >= ent[1]:
            ent[2] = []
        ent[1] += 16
        ins.then_inc(ent[0], 16)
        for t in ent[2]:
            t.val = ent[1]
        tk = T(ent[0], ent[1])
        ent[2].append(tk)
        self._commit(tk, reads, writes)

    def barrier(self, keep=None):
        keep = keep or (lambda k: False)
        t = [v for k, v in self.lastw.items() if not keep(k)]
        for k, d in self.readers.items():
            if not keep(k):
                t += list(d.values())
        for e in self.engs:
            e.wait(t)
        self.lastw = {k: v for k, v in self.lastw.items() if keep(k)}
        self.readers = {k: v for k, v in self.readers.items() if keep(k)}


def build(S):
    NT = S // 128
    NB = S // 512
    nc = bass.Bass("TRN2", target_bir_lowering=False)

    def din(name, shape, dt=F32):
        return nc.dram_tensor(name, list(shape), dt, kind="ExternalInput").ap()

    def dscr(name, shape, dt):
        return nc.dram_tensor(name, list(shape), dt, kind="Internal").ap()

    x_in = din("x", [S, D])
    a_w_in = din("a_w_in", [D, 3088])
    a_b_f = din("a_b_f", [1, 16])
    a_w_out = din("a_w_out", [D, D])
    b_w_q = din("b_w_q", [D, 1536])
    b_w_out = din("b_w_out", [512, D])
    w_kv = din("w_kv", [D, 3072])
    gains = din("gains", [7, D])
    ffn_w_up = din("ffn_w_up", [2, D, 2 * DFF])
    ffn_w_down = din("ffn_w_down", [2, DFF, D])
    cwb = din("cwb", [2, 128, NCH, 4])
    c_ident = din("c_ident", [128, 128], BF16)
    c_tri = din("c_tri", [128, 128], BF16)
    c_utri = din("c_utri", [128, 128])
    c_ones = din("c_ones", [128, 128])
    c_etab = din("c_etab", [24, 128, 256])
    out = nc.dram_tensor("out", [S, D], F32, kind="ExternalOutput").ap()

    QT = dscr("QT", [D, S], BF16)
    KT = dscr("KT", [D, S], BF16)
    VV = dscr("VV", [S, D], BF16)
    OT = dscr("OT", [D, S], BF16)
    X1 = dscr("X1", [S, D], F32)
    H1T = dscr("H1T", [D, S], BF16)
    X2 = dscr("X2", [S, D], F32)
    QTB = dscr("QTB", [1536, S], BF16)
    KTB = dscr("KTB", [1536, S], BF16)
    VB = dscr("VB", [3, S, 512], BF16)
    OTB = dscr("OTB", [512, S], BF16)
    X3 = dscr("X3", [S, D], F32)
    H3T = dscr("H3T", [D, S], BF16)

    es0 = contextlib.ExitStack()
    with es0:
        sems = [es0.enter_context(nc.semaphore(f"s{i}")) for i in range(100)]
        f = FW(nc, sems)
        PB = [es0.enter_context(nc.psum_tensor(f"pb{i}", [128, 512], F32)) for i in range(6)]
        PTB = [es0.enter_context(nc.psum_tensor(f"ptb{i}", [128, 1024], BF16)) for i in range(2)]

        uniq = [0]

        def sbt(es, name, shape, dt):
            uniq[0] += 1
            return es.enter_context(nc.sbuf_tensor(f"{name}_u{uniq[0]}", list(shape), dt))

        ident = sbt(es0, "ident", [128, 128], BF16)
        tri = sbt(es0, "tri", [128, 128], BF16)
        negh = sbt(es0, "negh", [128, 1], F32)
        esW0 = contextlib.ExitStack()
        Wdn0 = sbt(esW0, "Wdn0", [128, NJ, D], BF16)
        keepw_g = lambda k: k == "cw" or (isinstance(k, tuple) and k[0] in ("Wup", "Wdn"))
        esA = contextlib.ExitStack()
        BT = sbt(esA, "BT", [128, 16, NB, NT], F32)
        f.dma(f.sp, "c0", ident[:], c_ident[:, :], writes=["ident"])
        f.dma(f.sp, "c0", tri[:], c_tri[:, :], writes=["tri"])
        f.op(f.pool, lambda g: g.memset(negh[:], -0.5), writes=["negh"])

        def load_w(Wt, key, src, KC, N, CH=2048, sem="wl"):
            for kc in range(KC):
                for c0 in range(0, N, CH):
                    c1 = min(N, c0 + CH)
                    f.dma(f.pool, sem, Wt[:, kc, c0:c1], src[kc * 128:(kc + 1) * 128, c0:c1],
                          writes=[(key, kc)])

        def load_gain(Gt, key, row):
            f.dma(f.sp, "c0", Gt[:], gains[row:row + 1, :].partition_broadcast(128), writes=[key])

        def rstd_of(es_tmp, xt_ap, xkey, tmp):
            junk, ss, rs = tmp["junk"], tmp["ss"], tmp["rs"]
            f.op(f.dve, lambda v: v.scalar_tensor_tensor(out=junk[:], in0=xt_ap, scalar=1.0, in1=xt_ap,
                                                        op0=ALU.mult, op1=ALU.mult, accum_out=ss[:, 0:1]),
                 reads=[xkey], writes=["junk", "ss"])
            f.op(f.act, lambda a: a.activation(out=ss[:, 1:2], in_=ss[:, 0:1], func=AF.Sqrt, bias=EPS, scale=1.0 / D),
                 reads=["ss"], writes=["ss2"])
            f.op(f.dve, lambda v: v.reciprocal(rs[:], ss[:, 1:2]), reads=["ss2"], writes=["rs"])

        def mk_norm_tmp(es):
            return {"junk": sbt(es, "junk", [128, D], BF16), "ss": sbt(es, "ss", [128, 2], F32),
                    "rs": sbt(es, "rs", [128, 1], F32)}

        tcount = [0]

        def norm_T_store(xt_ap, xkey, tmp, Gt, gkey, hb, hbkey, hTs, hTskey, HTdst, tok0, semname):
            rs = tmp["rs"]
            f.op(f.dve, lambda v: v.scalar_tensor_tensor(out=hb[:], in0=xt_ap, scalar=rs[:, 0:1], in1=Gt[:],
                                                        op0=ALU.mult, op1=ALU.mult),
                 reads=[xkey, "rs", gkey], writes=[hbkey])
            pi = tcount[0] % 2
            tcount[0] += 1
            ptb = PTB[pi]
            for kc in range(8):
                f.mm(lambda t, kc=kc: t.transpose(ptb[:, kc * 128:(kc + 1) * 128], hb[:, kc * 128:(kc + 1) * 128], ident[:]),
                     reads=[hbkey, "ident"], writes=[f"PT{pi}"], last=(kc == 7))
            f.op(f.act, lambda a: a.copy(hTs[:].rearrange("p k t -> p (k t)"), ptb[:]), writes=[hTskey, f"PT{pi}"])
            f.dma(f.sp, semname, HTdst.rearrange("(k p) s -> p k s", p=128)[:, :, tok0:tok0 + 128], hTs[:],
                  reads=[hTskey], writes=[("HTdst", id(HTdst), tok0)])

        def pass_A1():
            with contextlib.ExitStack() as es:
                Win = sbt(es, "Win", [128, 8, 3088], BF16)
                G = sbt(es, "G0", [128, D], F32)
                Bf = sbt(es, "Bf", [128, 16], F32)
                utri = sbt(es, "utri", [128, 128], F32)
                ones = sbt(es, "ones", [128, 128], F32)
                xt = [sbt(es, f"xt{i}", [128, D], F32) for i in range(2)]
                hb = [sbt(es, f"hb{i}", [128, D], BF16) for i in range(2)]
                hTb = [sbt(es, f"hTb{i}", [128, 8, 512], BF16) for i in range(2)]
                qst = [sbt(es, f"qst{i}", [128, 512], BF16) for i in range(3)]
                vst = [sbt(es, f"vst{i}", [128, D], BF16) for i in range(2)]
                zall = sbt(es, "zall", [128, NT, 16], F32)
                dall = sbt(es, "dall", [128, NT, 16], F32)
                CW = sbt(es, "CW", [128, NT, 32], F32)
                carry = sbt(es, "carry", [128, NT + 1, 16], F32)
                Dfull = sbt(es, "Dfull", [128, NT, 16], F32)
                tmp = mk_norm_tmp(es)
                load_gain(G, "G", 0)
                f.dma(f.sp, "c0", Bf[:], a_b_f[0:1, :].partition_broadcast(128), writes=["Bf"])
                f.dma(f.sp, "c0", utri[:], c_utri[:, :], writes=["utri"])
                f.dma(f.sp, "c0", ones[:], c_ones[:, :], writes=["ones"])
                load_w(Win, "Win", a_w_in, 8, 3088)
                wkeys = [("Win", kc) for kc in range(8)]
                bank = [0]
                ev = [0]

                def evac(dst_ap, dkey, pbi, src_ap):
                    e = f.act if ev[0] % 2 == 0 else f.dve
                    ev[0] += 1
                    if e is f.act:
                        f.op(e, lambda a: a.copy(dst_ap, src_ap), writes=[dkey, f"PB{pbi}"])
                    else:
                        f.op(e, lambda v: v.tensor_copy(dst_ap, src_ap), writes=[dkey, f"PB{pbi}"])

                qs = [0]

                def ld_x(n):
                    f.dma(f.sp, f"xl{n % 2}", xt[n % 2][:], x_in[n * 128:(n + 1) * 128, :], writes=[f"xt{n % 2}"])

                ld_x(0)

                def norm_chain(n):
                    xs = xt[n % 2]
                    xk = f"xt{n % 2}"
                    if n + 1 < NT:
                        ld_x(n + 1)
                    rstd_of(es, xs[:], xk, tmp)
                    h = hb[n % 2]
                    hbk = f"hb{n % 2}"
                    rs = tmp["rs"]
                    f.op(f.dve, lambda v: v.scalar_tensor_tensor(out=h[:], in0=xs[:], scalar=rs[:, 0:1], in1=G[:],
                                                                op0=ALU.mult, op1=ALU.mult),
                         reads=[xk, "rs", "G"], writes=[hbk])

                def norm_trans(n):
                    b_, tl_ = n // 4, n % 4
                    hT_ = hTb[b_ % 2]
                    hk_ = f"hTb{b_ % 2}"
                    h = hb[n % 2]
                    hbk = f"hb{n % 2}"
                    pi = n % 2
                    ptb = PTB[pi]
                    for kc in range(8):
                        f.mm(lambda t, kc=kc: t.transpose(ptb[:, kc * 128:(kc + 1) * 128], h[:, kc * 128:(kc + 1) * 128], ident[:]),
                             reads=[hbk, "ident"], writes=[f"PT{pi}"], last=(kc == 7))
                    f.op(f.act, lambda a: a.copy(hT_[:, :, tl_ * 128:(tl_ + 1) * 128],
                                                 ptb[:].rearrange("p (k t) -> p k t", k=8)),
                         writes=[(hk_, tl_), f"PT{pi}"])

                def norm_tile(n):
                    norm_chain(n)
                    norm_trans(n)

                for tl in range(4):
                    norm_tile(tl)
                for b in range(NB):
                    hT = hTb[b % 2]
                    hk = f"hTb{b % 2}"
                    hkeys = [(hk, tl) for tl in range(4)]
                    for j in range(16):
                        pbi = bank[0] % 4
                        bank[0] += 1
                        for kc in range(8):
                            f.mm(lambda t, kc=kc, j=j, pbi=pbi: t.matmul(PB[pbi][:], lhsT=Win[:, kc, j * 128:(j + 1) * 128], rhs=hT[:, kc, :],
                                                                       start=(kc == 0), stop=(kc == 7)),
                                 reads=[wkeys[kc]] + hkeys, writes=[f"PB{pbi}"], last=(kc == 7))
                        st = qst[qs[0] % 3]
                        sk = f"qst{qs[0] % 3}"
                        qs[0] += 1
                        evac(st[:], sk, pbi, PB[pbi][:])
                        dst = QT if j < 8 else KT
                        jj = j % 8
                        f.dma(f.sp, sk, dst[jj * 128:(jj + 1) * 128, b * 512:(b + 1) * 512], st[:], reads=[sk],
                              writes=[("qk", j, b)])
                        if j % 4 == 0 and b + 1 < NB:
                            norm_chain((b + 1) * 4 + j // 4)
                        if j % 4 == 3 and b + 1 < NB:
                            norm_trans((b + 1) * 4 + j // 4)
                    for tl in range(4):
                        n = b * 4 + tl
                        vs = vst[n % 2]
                        vk = f"vst{n % 2}"
                        for half in range(2):
                            pbi = bank[0] % 4
                            bank[0] += 1
                            for kc in range(8):
                                f.mm(lambda t, kc=kc, half=half, pbi=pbi: t.matmul(PB[pbi][:], lhsT=hT[:, kc, tl * 128:(tl + 1) * 128],
                                                                                 rhs=Win[:, kc, 2048 + half * 512:2048 + (half + 1) * 512],
                                                                                 start=(kc == 0), stop=(kc == 7)),
                                     reads=[wkeys[kc], (hk, tl)], writes=[f"PB{pbi}"], last=(kc == 7))
                            evac(vs[:, half * 512:(half + 1) * 512], (vk, half), pbi, PB[pbi][:])
                        f.dma(f.sp, vk, VV[n * 128:(n + 1) * 128, :], vs[:], reads=[(vk, 0), (vk, 1)], writes=[("vv", n)])
                        for kc in range(8):
                            f.mm(lambda t, kc=kc: t.matmul(PB[4][:, 0:16], lhsT=hT[:, kc, tl * 128:(tl + 1) * 128], rhs=Win[:, kc, 3072:3088],
                                                           start=(kc == 0), stop=(kc == 7)),
                                 reads=[wkeys[kc], (hk, tl)], writes=["PB4"], last=(kc == 7))
                        f.op(f.dve, lambda v, n=n: v.tensor_tensor(out=zall[:, n, :], in0=PB[4][:, 0:16], in1=Bf[:], op=ALU.add),
                             reads=["Bf"], writes=[("zall", n), "PB4"])
                zk = [("zall", n) for n in range(NT)]
                f.op(f.act, lambda a: a.activation(out=zall[:], in_=zall[:], func=AF.Exp, scale=-1.0), reads=zk, writes=zk)
                f.op(f.act, lambda a: a.activation(out=dall[:], in_=zall[:], func=AF.Ln, bias=1.0, scale=1.0),
                     reads=zk, writes=[("dall", n) for n in range(NT)])
                for n in range(NT):
                    pb = PB[n // 16]
                    o = (n % 16) * 32
                    f.mm(lambda t, n=n, pb=pb, o=o: t.matmul(pb[:, o:o + 16], lhsT=utri[:], rhs=dall[:, n, :], start=True, stop=True),
                         reads=["utri", ("dall", n)], writes=[f"PB{n // 16}"], last=False)
                    f.mm(lambda t, n=n, pb=pb, o=o: t.matmul(pb[:, o + 16:o + 32], lhsT=ones[:], rhs=dall[:, n, :], start=True, stop=True),
                         reads=["ones", ("dall", n)], writes=[f"PB{n // 16}"], last=True)
                for hb_ in range((NT + 15) // 16):
                    n0 = hb_ * 16
                    n1 = min(NT, n0 + 16)
                    f.op(f.dve, lambda v, n0=n0, n1=n1, hb_=hb_: v.tensor_copy(CW[:, n0:n1, :].rearrange("p n c -> p (n c)"),
                                                                            PB[hb_][:, 0:(n1 - n0) * 32]),
                         writes=["CW", f"PB{hb_}"])
                f.op(f.dve, lambda v: v.memset(carry[:, 0, :], 0.0), writes=[("carry", 0)])
                for n in range(NT):
                    f.op(f.dve, lambda v, n=n: v.tensor_tensor(out=carry[:, n + 1, :], in0=carry[:, n, :], in1=CW[:, n, 16:32], op=ALU.add),
                         reads=[("carry", n), "CW"], writes=[("carry", n + 1)])
                f.op(f.dve, lambda v: v.tensor_tensor(out=Dfull[:], in0=CW[:, :, 0:16], in1=carry[:, 0:NT, :], op=ALU.add),
                     reads=["CW"] + [("carry", n) for n in range(NT + 1)], writes=["Dfull"])
                for h in range(16):
                    for Jb in range(NB):
                        f.op(f.dve, lambda v, h=h, Jb=Jb: v.tensor_scalar(out=BT[:, h, Jb, :], in0=Dfull[:, :, h],
                                                                        scalar1=carry[:, 4 * Jb + 2, h:h + 1], scalar2=None,
                                                                        op0=ALU.subtract),
                             reads=["Dfull", ("carry", 4 * Jb + 2)], writes=["BT"])
                f.barrier()

        def pass_A2():
            with contextlib.ExitStack() as es:
                KTp = [sbt(es, f"KTp{i}", [128, S], BF16) for i in range(2)]
                QTz = [[sbt(es, f"QTz{i}_{a}", [128, S], BF16) for a in range(2)] for i in range(2)]
                Va = [[sbt(es, f"Va{i}_{a}", [128, NT, 128], BF16) for a in range(2)] for i in range(2)]
                OTh = [[sbt(es, f"OTh{i}_{a}", [64, S], BF16) for a in range(2)] for i in range(2)]
                PTt = [sbt(es, f"PTt{i}", [128, 512], BF16) for i in range(6)]
                Rr = [sbt(es, f"Rr{i}", [64, 512], F32) for i in range(2)]
                for i in range(2):
                    for a in range(2):
                        f.op(f.pool, lambda g, i=i, a=a: g.memset(Va[i][a][:, :, 64:128], 1.0), writes=[(f"Va{i}_{a}", "ones")])
                        z0 = 64 if a == 0 else 0
                        f.op(f.pool, lambda g, i=i, a=a, z0=z0: g.memset(QTz[i][a][z0:z0 + 64, :], 0.0), writes=[(f"QTz{i}_{a}", "z")])
                VVr = VV.rearrange("(n p) c -> p n c", p=128)

                def load_pair(hp):
                    s = hp % 2
                    f.dma(f.sp, f"hk{s}", KTp[s][:], KT[hp * 128:(hp + 1) * 128, :], writes=[f"KTp{s}"])
                    for a in range(2):
                        h = 2 * hp + a
                        q0 = 0 if a == 0 else 64
                        f.dma(f.sp, f"hq{s}", QTz[s][a][q0:q0 + 64, :], QT[h * 64:(h + 1) * 64, :], writes=[(f"QTz{s}_{a}", "q")])
                        f.dma(f.sp, f"hv{s}", Va[s][a][:, :, 0:64], VVr[:, :, h * 64:(h + 1) * 64], writes=[(f"Va{s}_{a}", "v")])

                load_pair(0)
                ucount = [0]
                oac = [0]
                LA = 3
                for hp in range(NH_A // 2):
                    if hp + 1 < NH_A // 2:
                        load_pair(hp + 1)
                    s = hp % 2
                    for a in range(2):
                        h = 2 * hp + a
                        units = [(Jb, i) for Jb in range(NB) for i in range(4 * Jb + 4)]
                        info = {}
                        Qz, Vh, Oh_ = QTz[s][a], Va[s][a], OTh[s][a]
                        qkeys = [(f"QTz{s}_{a}", "q"), (f"QTz{s}_{a}", "z"), f"KTp{s}"]
                        vkeys = [(f"Va{s}_{a}", "v"), (f"Va{s}_{a}", "ones")]
                        okey = f"OTh{s}_{a}"

                        def emit_qk(u):
                            Jb, i = u
                            il = i - 4 * Jb
                            c0 = 128 * il if il > 0 else 0
                            uid = ucount[0]
                            ucount[0] += 1
                            sb_ = uid % 4
                            pt = uid % 6
                            info[u] = (c0, pt)
                            f.mm(lambda t: t.matmul(PB[sb_][:, c0:512], lhsT=KTp[s][:, i * 128:(i + 1) * 128],
                                                    rhs=Qz[:, Jb * 512 + c0:(Jb + 1) * 512], start=True, stop=True),
                                 reads=qkeys, writes=[f"PB{sb_}"], last=True)
                            f.op(f.act, lambda a_: a_.activation(out=PTt[pt][:, c0:512], in_=PB[sb_][:, c0:512], func=AF.Exp,
                                                                 bias=BT[:, h, Jb, i:i + 1], scale=0.125),
                                 reads=["BT"], writes=[f"PTt{pt}", f"PB{sb_}"])
                            if il >= 0:
                                f.op(f.dve, lambda g: g.tensor_tensor(out=PTt[pt][:, c0:c0 + 128], in0=PTt[pt][:, c0:c0 + 128], in1=tri[:],
                                                                       op=ALU.mult),
                                     reads=["tri"], writes=[f"PTt{pt}"])

                        def emit_pv(u):
                            Jb, i = u
                            c0, pt = info[u]
                            ob = 4 + (oac[0] % 2)
                            lasti = (i == 4 * Jb + 3)
                            f.mm(lambda t: t.matmul(PB[ob][:, c0:512], lhsT=Vh[:, i, :], rhs=PTt[pt][:, c0:512],
                                                    start=(i == 0), stop=lasti),
                                 reads=vkeys + [f"PTt{pt}"], writes=[f"PB{ob}"], last=lasti)
                            if lasti:
                                r = Rr[oac[0] % 2]
                                rk = f"Rr{oac[0] % 2}"
                                f.op(f.dve, lambda v: v.reciprocal(r[:], PB[ob][64:128, :]), writes=[rk, f"PB{ob}"])
                                f.op(f.dve, lambda v: v.tensor_tensor(out=Oh_[:, Jb * 512:(Jb + 1) * 512], in0=PB[ob][0:64, :], in1=r[:],
                                                                      op=ALU.mult),
                                     reads=[rk], writes=[(okey, Jb), f"PB{ob}"])
                                oac[0] += 1

                        for idx in range(len(units) + LA):
                            if idx < len(units):
                                emit_qk(units[idx])
                            if idx - LA >= 0:
                                emit_pv(units[idx - LA])
                        f.dma(f.sp, f"ho{s}", OT[h * 64:(h + 1) * 64, :], Oh_[:], reads=[(okey, Jb) for Jb in range(NB)],
                              writes=[("OT", h)])
                f.barrier(keep=keepw_g)

        def pass_OF(layer, OTsrc, KC, Wsrc, Xin, Xmid, grow, HT, Xout, final, Wdn_pre=None):
            with contextlib.ExitStack() as esw:
                Wup = sbt(esw, "Wup", [128, 8, 2 * DFF], BF16)
                Wdn = Wdn_pre if Wdn_pre is not None else sbt(esw, "Wdn", [128, NJ, D], BF16)
                cw = sbt(esw, "cw", [128, NCH, 4], F32)
                keepw = lambda k: k == "cw" or (isinstance(k, tuple) and k[0] in ("Wup", "Wdn"))
                with contextlib.ExitStack() as es:
                    W = sbt(es, "Wo", [128, KC, D], BF16)
                    G = sbt(es, "Go", [128, D], F32)
                    ob = [sbt(es, f"ob{i}", [128, KC, 512], BF16) for i in range(2)]
                    xt = [sbt(es, f"xt{i}", [128, D], F32) for i in range(3)]
                    xn = [sbt(es, f"xn{i}", [128, D], F32) for i in range(3)]
                    hb = [sbt(es, f"hb{i}", [128, D], BF16) for i in range(2)]
                    hTs = [sbt(es, f"hTs{i}", [128, 8, 128], BF16) for i in range(2)]
                    tmp = mk_norm_tmp(es)
                    load_gain(G, "G", grow)
                    load_w(W, "W", Wsrc, KC, D)
                    f.dma(f.sp, "c0", cw[:], cwb[layer], writes=["cw"])
                    load_w(Wup, "Wup", ffn_w_up[layer], 8, 2 * DFF, sem="wlu")
                    if Wdn_pre is None:
                        load_w(Wdn, "Wdn", ffn_w_down[layer], NJ, D, sem="wld")
                    OTr = OTsrc.rearrange("(k p) s -> p k s", p=128)
                    bank = [0]

                    def ld_ob(b):
                        f.dma(f.sp, f"ol{b % 2}", ob[b % 2][:], OTr[:, :, b * 512:(b + 1) * 512], writes=[f"ob{b % 2}"])

                    def ld_x(n):
                        f.dma(f.sp, f"xl{n % 3}", xt[n % 3][:], Xin[n * 128:(n + 1) * 128, :], writes=[f"xt{n % 3}"])

                    ld_ob(0)
                    ld_x(0)

                    def mm_tile(n):
                        b, tl = n // 4, n % 4
                        o = ob[b % 2]
                        ok = f"ob{b % 2}"
                        if tl == 0 and b + 1 < NB:
                            ld_ob(b + 1)
                        xs, xk = xt[n % 3], f"xt{n % 3}"
                        if n + 1 < NT:
                            ld_x(n + 1)
                        xo, xok = xn[n % 3], f"xn{n % 3}"
                        for half in range(2):
                            pbi = bank[0] % 4
                            bank[0] += 1
                            for kc in range(KC):
                                f.mm(lambda t, kc=kc, half=half, pbi=pbi: t.matmul(PB[pbi][:], lhsT=o[:, kc, tl * 128:(tl + 1) * 128],
                                                                                 rhs=W[:, kc, half * 512:(half + 1) * 512],
                                                                                 start=(kc == 0), stop=(kc == KC - 1)),
                                     reads=[("W", kc), ok], writes=[f"PB{pbi}"], last=(kc == KC - 1))
                            f.op(f.dve, lambda v, half=half, pbi=pbi: v.tensor_tensor(out=xo[:, half * 512:(half + 1) * 512], in0=PB[pbi][:],
                                                                                    in1=xs[:, half * 512:(half + 1) * 512], op=ALU.add),
                                 reads=[xk], writes=[(xok, half), f"PB{pbi}"])
                        f.dma(f.sp, f"xs{n % 3}", Xmid[n * 128:(n + 1) * 128, :], xo[:], reads=[(xok, 0), (xok, 1)], writes=[("Xmid", n)])

                    def norm_tile(n):
                        xo, xok = xn[n % 3], f"xn{n % 3}"
                        junk, ss, rs = tmp["junk"], tmp["ss"], tmp["rs"]
                        f.op(f.dve, lambda v: v.scalar_tensor_tensor(out=junk[:], in0=xo[:], scalar=1.0, in1=xo[:], op0=ALU.mult,
                                                                    op1=ALU.mult, accum_out=ss[:, 0:1]),
                             reads=[(xok, 0), (xok, 1)], writes=["junk", "ss"])
                        f.op(f.act, lambda a: a.activation(out=ss[:, 1:2], in_=ss[:, 0:1], func=AF.Sqrt, bias=EPS, scale=1.0 / D),
                             reads=["ss"], writes=["ss2"])
                        f.op(f.dve, lambda v: v.reciprocal(rs[:], ss[:, 1:2]), reads=["ss2"], writes=["rs"])
                        h, hbk = hb[n % 2], f"hb{n % 2}"
                        f.op(f.dve, lambda v: v.scalar_tensor_tensor(out=h[:], in0=xo[:], scalar=rs[:, 0:1], in1=G[:], op0=ALU.mult,
                                                                    op1=ALU.mult),
                             reads=[(xok, 0), (xok, 1), "rs", "G"], writes=[hbk])
                        pi = n % 2
                        ptb = PTB[pi]
                        for kc in range(8):
                            f.mm(lambda t, kc=kc: t.transpose(ptb[:, kc * 128:(kc + 1) * 128], h[:, kc * 128:(kc + 1) * 128], ident[:]),
                                 reads=[hbk, "ident"], writes=[f"PT{pi}"], last=(kc == 7))
                        hs, hsk = hTs[n % 2], f"hTs{n % 2}"
                        f.op(f.act, lambda a: a.copy(hs[:].rearrange("p k t -> p (k t)"), ptb[:]), writes=[hsk, f"PT{pi}"])
                        f.dma(f.sp, f"hs{n % 2}", HT.rearrange("(k p) s -> p k s", p=128)[:, :, n * 128:(n + 1) * 128], hs[:],
                              reads=[hsk], writes=[("HT", n)])

                    mm_tile(0)
                    mm_tile(1)
                    for n in range(NT):
                        if n + 2 < NT:
                            mm_tile(n + 2)
                        norm_tile(n)
                    f.barrier(keep=keepw)
                with contextlib.ExitStack() as es:
                    MT = sbt(es, "MT", [128, NJ, 512], BF16)
                    NHB = 2
                    hTb = [sbt(es, f"hTf{i}", [128, 8, 512], BF16) for i in range(NHB)]
                    ub = [sbt(es, f"ub{i}", [128, 516], F32) for i in range(3)]
                    tails = sbt(es, "tails", [128, NCH, 2], F32)
                    cv = [[sbt(es, f"cv{w_}_{i}", [128, 512], F32) for i in range(3)] for w_ in range(2)]
                    sg = [sbt(es, f"sg{i}", [128, 512], F32) for i in range(2)]
                    xt = [sbt(es, f"xtf{i}", [128, D], F32) for i in range(2)]
                    if final:
                        G = sbt(es, "Gf", [128, D], F32)
                        tmp = mk_norm_tmp(es)
                        load_gain(G, "G", 5)
                    f.op(f.dve, lambda v: v.memset(tails[:], 0.0), writes=[("tails", c) for c in range(NCH)])
                    HTr = HT.rearrange("(k p) s -> p k s", p=128)
                    ubank = [0]
                    pbank = [0]

                    def ld_h(b):
                        f.dma(f.sp, f"hl{b % NHB}", hTb[b % NHB][:], HTr[:, :, b * 512:(b + 1) * 512], writes=[f"hT{b % NHB}"])

                    def ld_x(n):
                        f.dma(f.sp, f"xl{n % 2}", xt[n % 2][:], Xmid[n * 128:(n + 1) * 128, :], writes=[f"xt{n % 2}"])

                    ld_h(0)
                    ld_x(0)

                    def gate_mult(j, cvs):
                        sgt, sgk = sg[j % 2], f"sg{j % 2}"
                        f.op(f.act, lambda a: a.activation(out=sgt[:], in_=cvs[1][0][:], func=AF.Silu), reads=[cvs[1][1]], writes=[sgk])
                        me = f.dve if j % 2 == 0 else f.pool
                        f.op(me, lambda g: g.tensor_tensor(out=MT[:, j, :], in0=sgt[:], in1=cvs[0][0][:], op=ALU.mult),
                             reads=[sgk, cvs[0][1]], writes=[("MT", j)])

                    def up_pair(b, j):
                        hT, hk = hTb[b % NHB], f"hT{b % NHB}"
                        cvs = []
                        for which in range(2):
                            c = j + which * NJ
                            pbi = pbank[0] % 6
                            pbank[0] += 1
                            u = ub[ubank[0] % 3]
                            uk = f"ub{ubank[0] % 3}"
                            ubank[0] += 1
                            for kc in range(8):
                                f.mm(lambda t, kc=kc, c=c, pbi=pbi: t.matmul(PB[pbi][:], lhsT=Wup[:, kc, c * 128:(c + 1) * 128], rhs=hT[:, kc, :],
                                                                          start=(kc == 0), stop=(kc == 7)),
                                     reads=[("Wup", kc), hk], writes=[f"PB{pbi}"], last=(kc == 7))
                            f.op(f.act, lambda a, u=u, pbi=pbi: a.copy(u[:, 2:514], PB[pbi][:]), writes=[(uk, "b"), f"PB{pbi}"])
                            f.op(f.dve, lambda g, c=c, u=u: g.tensor_copy(u[:, 0:2], tails[:, c, :]), reads=[("tails", c)], writes=[(uk, "h")])
                            f.op(f.dve, lambda g, c=c, u=u: g.tensor_copy(tails[:, c, :], u[:, 512:514]), reads=[(uk, "b")], writes=[("tails", c)])
                            cvt = cv[which][j % 3]
                            ck = f"cv{which}_{j % 3}"
                            cvs.append((cvt, ck))
                            f.op(f.act, lambda a, c=c, u=u, cvt=cvt: a.activation(out=cvt[:], in_=u[:, 2:514], func=AF.Identity,
                                                                               bias=cw[:, c, 3:4], scale=cw[:, c, 2:3]),
                                 reads=[(uk, "b"), "cw"], writes=[ck])
                            f.op(f.dve, lambda v, c=c, u=u, cvt=cvt: v.scalar_tensor_tensor(out=cvt[:], in0=u[:, 1:513], scalar=cw[:, c, 1:2],
                                                                                         in1=cvt[:], op0=ALU.mult, op1=ALU.add),
                                 reads=[(uk, "b"), (uk, "h"), "cw", ck], writes=[ck])
                            f.op(f.dve, lambda v, c=c, u=u, cvt=cvt: v.scalar_tensor_tensor(out=cvt[:], in0=u[:, 0:512], scalar=cw[:, c, 0:1],
                                                                                         in1=cvt[:], op0=ALU.mult, op1=ALU.add),
                                 reads=[(uk, "b"), (uk, "h"), "cw", ck], writes=[ck])
                        return cvs

                    PRE = 2
                    pend = None
                    for b in range(NB):
                        if b + 1 < NB:
                            ld_h(b + 1)
                        for j in range(PRE if b > 0 else 0, NJ):
                            cvs = up_pair(b, j)
                            if pend is not None:
                                gate_mult(*pend)
                            pend = (j, cvs)
                        gate_mult(*pend)
                        pend = None
                        held = []
                        if b + 1 < NB:
                            for j in range(PRE):
                                held.append((j, up_pair(b + 1, j)))
                        mkeys = [("MT", j) for j in range(NJ)]
                        for tl in range(4):
                            n = b * 4 + tl
                            xs, xk = xt[n % 2], f"xt{n % 2}"
                            if n + 1 < NT:
                                ld_x(n + 1)
                            for half in range(2):
                                pbi = pbank[0] % 6
                                pbank[0] += 1
                                for j in range(NJ):
                                    f.mm(lambda t, j=j, half=half, pbi=pbi: t.matmul(PB[pbi][:], lhsT=MT[:, j, tl * 128:(tl + 1) * 128],
                                                                                   rhs=Wdn[:, j, half * 512:(half + 1) * 512],
                                                                                   start=(j == 0), stop=(j == NJ - 1)),
                                         reads=[("Wdn", j), mkeys[j]], writes=[f"PB{pbi}"], last=(j == NJ - 1))
                                f.op(f.dve, lambda v, half=half, pbi=pbi: v.tensor_tensor(out=xs[:, half * 512:(half + 1) * 512], in0=PB[pbi][:],
                                                                                        in1=xs[:, half * 512:(half + 1) * 512], op=ALU.add),
                                     reads=[], writes=[xk, f"PB{pbi}"])
                            if not final:
                                f.dma(f.sp, f"xsf{n % 2}", Xout[n * 128:(n + 1) * 128, :], xs[:], reads=[xk], writes=[("Xout", n)])
                            else:
                                junk, ss, rs = tmp["junk"], tmp["ss"], tmp["rs"]
                                f.op(f.dve, lambda v: v.scalar_tensor_tensor(out=junk[:], in0=xs[:], scalar=1.0, in1=xs[:], op0=ALU.mult,
                                                                            op1=ALU.mult, accum_out=ss[:, 0:1]),
                                     reads=[xk], writes=["junk", "ss"])
                                f.op(f.act, lambda a: a.activation(out=ss[:, 1:2], in_=ss[:, 0:1], func=AF.Sqrt, bias=EPS, scale=1.0 / D),
                                     reads=["ss"], writes=["ss2"])
                                f.op(f.dve, lambda v: v.reciprocal(rs[:], ss[:, 1:2]), reads=["ss2"], writes=["rs"])
                                f.op(f.dve, lambda v: v.scalar_tensor_tensor(out=xs[:], in0=xs[:], scalar=rs[:, 0:1], in1=G[:], op0=ALU.mult,
                                                                            op1=ALU.mult),
                                     reads=["rs", "G"], writes=[xk])
                                f.dma(f.sp, f"xsf{n % 2}", out[n * 128:(n + 1) * 128, :], xs[:], reads=[xk], writes=[("out", n)])
                        for hj in held[:-1]:
                            gate_mult(*hj)
                        if held:
                            pend = held[-1]
                    f.barrier()

        def pass_B1():
            NSB = S // 2048
            with contextlib.ExitStack() as es:
                Wkv = sbt(es, "Wkv", [128, 8, 3072], BF16)
                Wq = sbt(es, "Wq", [128, 8, 1536], BF16)
                Gkv = sbt(es, "Gkv", [128, D], F32)
                Gq = sbt(es, "Gq", [128, D], F32)
                xt = [sbt(es, f"xt{i}", [128, D], F32) for i in range(2)]
                hb = [[sbt(es, f"hb{w_}_{i}", [128, D], BF16) for i in range(2)] for w_ in range(2)]
                hkvT = sbt(es, "hkvT", [128, 8, 2048], BF16)
                hqTb = [sbt(es, f"hqT{i}", [128, 8, 512], BF16) for i in range(2)]
                st = [sbt(es, f"st{i}", [128, 512], BF16) for i in range(3)]
                st2 = sbt(es, "st2", [128, 8, 2048], BF16)
                vst = [sbt(es, f"vst{i}", [128, 512], BF16) for i in range(2)]
                tmp = mk_norm_tmp(es)
                load_gain(Gkv, "Gkv", 4)
                load_gain(Gq, "Gq", 1)
                load_w(Wq, "Wq", b_w_q, 8, 1536, sem="wlq")
                load_w(Wkv, "Wkv", w_kv, 8, 3072, sem="wlk")
                wkv_keys = [("Wkv", kc) for kc in range(8)]
                wq_keys = [("Wq", kc) for kc in range(8)]
                bank = [0]
                ev = [0]
                sc = [0]
                vc = [0]
                tc = [0]

                def evac(dst_ap, dkey, pbi, src_ap, extra_reads=()):
                    e = f.act if ev[0] % 2 == 0 else f.dve
                    ev[0] += 1
                    if e is f.act:
                        f.op(e, lambda a: a.copy(dst_ap, src_ap), reads=list(extra_reads), writes=[dkey, f"PB{pbi}"])
                    else:
                        f.op(e, lambda v: v.tensor_copy(dst_ap, src_ap), reads=list(extra_reads), writes=[dkey, f"PB{pbi}"])

                def ld_x(n):
                    f.dma(f.sp, f"xl{n % 2}", xt[n % 2][:], X2[n * 128:(n + 1) * 128, :], writes=[f"xt{n % 2}"])

                ld_x(0)
                Gs = ((Gkv, "Gkv"), (Gq, "Gq"))

                def norm_chain(n):
                    xs, xk = xt[n % 2], f"xt{n % 2}"
                    if n + 1 < NT:
                        ld_x(n + 1)
                    rstd_of(es, xs[:], xk, tmp)
                    rs = tmp["rs"]
                    for which, (Gt, gk) in enumerate(Gs):
                        h, hbk = hb[which][n % 2], f"hb{which}_{n % 2}"
                        f.op(f.dve, lambda v, h=h, Gt=Gt: v.scalar_tensor_tensor(out=h[:], in0=xs[:], scalar=rs[:, 0:1], in1=Gt[:],
                                                                              op0=ALU.mult, op1=ALU.mult),
                             reads=[xk, "rs", gk], writes=[hbk])

                def norm_trans(n):
                    b_ = n // 4
                    bl_, tl_ = b_ % 4, n % 4
                    for which in range(2):
                        h, hbk = hb[which][n % 2], f"hb{which}_{n % 2}"
                        pi = tc[0] % 2
                        tc[0] += 1
                        ptb = PTB[pi]
                        for kc in range(8):
                            f.mm(lambda t, kc=kc, h=h, ptb=ptb: t.transpose(ptb[:, kc * 128:(kc + 1) * 128], h[:, kc * 128:(kc + 1) * 128],
                                                                          ident[:]),
                                 reads=[hbk, "ident"], writes=[f"PT{pi}"], last=(kc == 7))
                        if which == 0:
                            c0 = bl_ * 512 + tl_ * 128
                            f.op(f.act, lambda a, c0=c0, ptb=ptb: a.copy(hkvT[:, :, c0:c0 + 128], ptb[:].rearrange("p (k t) -> p k t", k=8)),
                                 writes=[("hkvT", bl_, tl_), f"PT{pi}"])
                        else:
                            hq_ = hqTb[b_ % 2]
                            f.op(f.act, lambda a, ptb=ptb, hq_=hq_: a.copy(hq_[:, :, tl_ * 128:(tl_ + 1) * 128],
                                                                         ptb[:].rearrange("p (k t) -> p k t", k=8)),
                                 writes=[(f"hqT{b_ % 2}", tl_), f"PT{pi}"])

                gcount = [0]

                def maybe_next(b_next):
                    gi = gcount[0]
                    gcount[0] += 1
                    if b_next is None:
                        return
                    if gi % 8 == 1:
                        norm_chain(b_next * 4 + gi // 8)
                    if gi % 8 == 6:
                        norm_trans(b_next * 4 + gi // 8)

                for sbk in range(NSB):
                    for tl in range(4):
                        norm_chain(sbk * 16 + tl)
                        norm_trans(sbk * 16 + tl)
                    for bl in range(4):
                        b = sbk * 4 + bl
                        hqT = hqTb[b % 2]
                        gcount[0] = 0
                        b_next = b + 1 if bl < 3 else None
                        kvkeys = [("hkvT", bl, tl) for tl in range(4)]
                        qkeys = [(f"hqT{b % 2}", tl) for tl in range(4)]
                        for isk in range(2):
                            for p in range(12):
                                g = p // 4
                                pbi = bank[0] % 4
                                bank[0] += 1
                                for kc in range(8):
                                    if isk:
                                        f.mm(lambda t, kc=kc, p=p, pbi=pbi: t.matmul(PB[pbi][:], lhsT=Wkv[:, kc, p * 128:(p + 1) * 128],
                                                                                  rhs=hkvT[:, kc, bl * 512:(bl + 1) * 512],
                                                                                  start=(kc == 0), stop=(kc == 7)),
                                             reads=[wkv_keys[kc]] + kvkeys, writes=[f"PB{pbi}"], last=(kc == 7))
                                    else:
                                        f.mm(lambda t, kc=kc, p=p, pbi=pbi: t.matmul(PB[pbi][:], lhsT=Wq[:, kc, p * 128:(p + 1) * 128],
                                                                                  rhs=hqT[:, kc, :], start=(kc == 0), stop=(kc == 7)),
                                             reads=[wq_keys[kc]] + qkeys, writes=[f"PB{pbi}"], last=(kc == 7))
                                dst = KTB if isk else QTB
                                if g == 0:
                                    s_, sk = st[sc[0] % 3], f"st{sc[0] % 3}"
                                    sc[0] += 1
                                    evac(s_[:], sk, pbi, PB[pbi][:])
                                    f.dma(f.sp, sk, dst[p * 128:(p + 1) * 128, b * 512:(b + 1) * 512], s_[:], reads=[sk], writes=[("qkb", isk, p, b)])
                                elif g == 1:
                                    s_, sk = st[sc[0] % 3], f"st{sc[0] % 3}"
                                    sc[0] += 1
                                    evac(s_[:].rearrange("p (r i) -> p r i", r=4), sk, pbi, PB[pbi][:].rearrange("p (i r) -> p r i", r=4))
                                    f.dma(f.sp, sk, dst[p * 128:(p + 1) * 128, b * 512:(b + 1) * 512], s_[:], reads=[sk], writes=[("qkb", isk, p, b)])
                                else:
                                    q = isk * 4 + (p - 8)
                                    dv = st2[:, q, :].rearrange("p (r i) -> p r i", r=16)[:, :, bl * 32:(bl + 1) * 32]
                                    evac(dv, ("st2", q, bl), pbi, PB[pbi][:].rearrange("p (i r) -> p r i", r=16))
                                    if bl == 3:
                                        f.dma(f.sp, "st2", dst[p * 128:(p + 1) * 128, sbk * 2048:(sbk + 1) * 2048], st2[:, q, :],
                                              reads=[("st2", q, k_) for k_ in range(4)], writes=[("qkb", isk, p, sbk)])
                                maybe_next(b_next)
                        for g in range(2):
                            dil = B_GROUPS[g][1]
                            for r in range(4):
                                if g == 0:
                                    lh = lambda kc, r=r: hkvT[:, kc, bl * 512 + r * 128: bl * 512 + (r + 1) * 128]
                                else:
                                    lh = lambda kc, r=r: hkvT[:, kc, bl * 512 + r: bl * 512 + 512: 4]
                                pbi = bank[0] % 4
                                bank[0] += 1
                                for kc in range(8):
                                    f.mm(lambda t, kc=kc, lh=lh, pbi=pbi, g=g: t.matmul(PB[pbi][:], lhsT=lh(kc),
                                                                                     rhs=Wkv[:, kc, 1536 + g * 512:1536 + (g + 1) * 512],
                                                                                     start=(kc == 0), stop=(kc == 7)),
                                         reads=[wkv_keys[kc]] + kvkeys, writes=[f"PB{pbi}"], last=(kc == 7))
                                v_, vk = vst[vc[0] % 2], f"vst{vc[0] % 2}"
                                vc[0] += 1
                                evac(v_[:], vk, pbi, PB[pbi][:])
                                tau = b * 4 + r
                                f.dma(f.sp, vk, VB[g, tau * 128:(tau + 1) * 128, :], v_[:], reads=[vk], writes=[("vb", g, tau)])
                                maybe_next(b_next)
                    allkv = [("hkvT", bl, tl) for bl in range(4) for tl in range(4)]
                    for r in range(16):
                        pbi = bank[0] % 4
                        bank[0] += 1
                        for kc in range(8):
                            f.mm(lambda t, kc=kc, r=r, pbi=pbi: t.matmul(PB[pbi][:], lhsT=hkvT[:, kc, r:2048:16],
                                                                      rhs=Wkv[:, kc, 1536 + 1024:1536 + 1536],
                                                                      start=(kc == 0), stop=(kc == 7)),
                                 reads=[wkv_keys[kc]] + allkv, writes=[f"PB{pbi}"], last=(kc == 7))
                        v_, vk = vst[vc[0] % 2], f"vst{vc[0] % 2}"
                        vc[0] += 1
                        evac(v_[:], vk, pbi, PB[pbi][:])
                        tau = sbk * 16 + r
                        f.dma(f.sp, vk, VB[2, tau * 128:(tau + 1) * 128, :], v_[:], reads=[vk], writes=[("vb", 2, tau)])
                f.barrier()

        def pass_B2():
            with contextlib.ExitStack() as es:
                Qg = [[sbt(es, f"Qg{s}_{g}", [128, S], BF16) for g in range(3)] for s in range(2)]
                Kg = [[sbt(es, f"Kg{s}_{g}", [128, S], BF16) for g in range(3)] for s in range(2)]
                Vg = [[sbt(es, f"Vg{s}_{g}", [128, NT, 128], BF16) for g in range(3)] for s in range(2)]
                Et = [sbt(es, f"Et{s}", [128, 3, 256], F32) for s in range(2)]
                Nacc = sbt(es, "Nacc", [128, S], F32)
                Pf = [sbt(es, f"Pf{i}", [128, 256], F32) for i in range(4)]
                Pb = [sbt(es, f"Pb{i}", [128, 256], BF16) for i in range(6)]
                Rr = sbt(es, "Rb", [64, S], F32)
                Oh = [sbt(es, f"Oh{i}", [64, S], BF16) for i in range(2)]
                for s in range(2):
                    for g in range(3):
                        f.op(f.pool, lambda gp, s=s, g=g: gp.memset(Vg[s][g][:, :, 64:128], 1.0), writes=[(f"Vg{s}_{g}", "ones")])
                        f.op(f.pool, lambda gp, s=s, g=g: gp.memset(Qg[s][g][64:128, :], 0.0), writes=[(f"Qg{s}_{g}", "z")])
                        f.op(f.pool, lambda gp, s=s, g=g: gp.memset(Kg[s][g][64:128, :], 0.0), writes=[(f"Kg{s}_{g}", "z")])

                def load_head(hh):
                    s = hh % 2
                    for g in range(3):
                        r0 = g * 512 + hh * 64
                        f.dma(f.sp, f"bq{s}", Qg[s][g][0:64, :], QTB[r0:r0 + 64, :], writes=[f"Qg{s}_{g}"])
                        f.dma(f.sp, f"bk{s}", Kg[s][g][0:64, :], KTB[r0:r0 + 64, :], writes=[f"Kg{s}_{g}"])
                        f.dma(f.sp, f"bv{s}", Vg[s][g][:, :, 0:64], VB[g].rearrange("(n p) c -> p n c", p=128)[:, :, hh * 64:(hh + 1) * 64],
                              writes=[(f"Vg{s}_{g}", "v")])
                        f.dma(f.sp, f"be{s}", Et[s][:, g, :], c_etab[g * 8 + hh], writes=[(f"Et{s}", g)])

                load_head(0)
                uc = [0]
                LA = 3
                for hh in range(8):
                    if hh + 1 < 8:
                        load_head(hh + 1)
                    s = hh % 2
                    units = [(g, tau) for g in range(3) for tau in range(NT)]
                    info = {}

                    def emit_qk(u):
                        g, tau = u
                        dil = B_GROUPS[g][1]
                        prev = tau - dil
                        w = 256 if prev >= 0 else 128
                        uid = uc[0]
                        uc[0] += 1
                        sb_ = uid % 4
                        info[u] = (w, uid)
                        Q, K = Qg[s][g], Kg[s][g]
                        f.mm(lambda t: t.matmul(PB[sb_][:, 0:128], lhsT=K[:, tau * 128:(tau + 1) * 128], rhs=Q[:, tau * 128:(tau + 1) * 128],
                                                start=True, stop=True),
                             reads=[f"Qg{s}_{g}", f"Kg{s}_{g}", (f"Qg{s}_{g}", "z"), (f"Kg{s}_{g}", "z")], writes=[f"PB{sb_}"], last=(prev < 0))
                        if prev >= 0:
                            f.mm(lambda t: t.matmul(PB[sb_][:, 128:256], lhsT=K[:, prev * 128:(prev + 1) * 128], rhs=Q[:, tau * 128:(tau + 1) * 128],
                                                    start=True, stop=True),
                                 reads=[f"Qg{s}_{g}", f"Kg{s}_{g}", (f"Qg{s}_{g}", "z"), (f"Kg{s}_{g}", "z")], writes=[f"PB{sb_}"], last=True)
                        pf, pfk = Pf[uid % 4], f"Pf{uid % 4}"
                        f.op(f.act, lambda a: a.activation(out=pf[:, 0:w], in_=PB[sb_][:, 0:w], func=AF.Exp, scale=0.125),
                             writes=[pfk, f"PB{sb_}"])
                        pb, pbk = Pb[uid % 6], f"Pb{uid % 6}"
                        em = f.dve if uid % 3 == 0 else f.pool
                        f.op(em, lambda v: v.tensor_tensor(out=pb[:, 0:w], in0=pf[:, 0:w], in1=Et[s][:, g, 0:w], op=ALU.mult),
                             reads=[pfk, (f"Et{s}", g)], writes=[pbk])

                    def emit_pv(u):
                        g, tau = u
                        dil = B_GROUPS[g][1]
                        prev = tau - dil
                        w, uid = info[u]
                        ob = 4 + (uid % 2)
                        pb, pbk = Pb[uid % 6], f"Pb{uid % 6}"
                        V = Vg[s][g]
                        vks = [(f"Vg{s}_{g}", "v"), (f"Vg{s}_{g}", "ones")]
                        f.mm(lambda t: t.matmul(PB[ob][:, 0:128], lhsT=V[:, tau, :], rhs=pb[:, 0:128], start=True, stop=(prev < 0)),
                             reads=vks + [pbk], writes=[f"PB{ob}"], last=(prev < 0))
                        if prev >= 0:
                            f.mm(lambda t: t.matmul(PB[ob][:, 0:128], lhsT=V[:, prev, :], rhs=pb[:, 128:256], start=False, stop=True),
                                 reads=vks + [pbk], writes=[f"PB{ob}"], last=True)
                        n_, r_ = tau // dil, tau % dil
                        base = n_ * 128 * dil + r_
                        dstv = Nacc[:, base: base + 127 * dil + 1: dil]
                        if g == 0:
                            f.op(f.dve, lambda v: v.tensor_copy(dstv, PB[ob][:, 0:128]), writes=[("Nacc", tau // 16), f"PB{ob}"])
                        else:
                            f.op(f.dve, lambda v: v.tensor_tensor(out=dstv, in0=PB[ob][:, 0:128], in1=dstv, op=ALU.add),
                                 writes=[("Nacc", (n_ * dil) // 16), f"PB{ob}"])

                    for idx in range(len(units) + LA):
                        if idx < len(units):
                            emit_qk(units[idx])
                        if idx - LA >= 0:
                            emit_pv(units[idx - LA])
                    nkeys = [("Nacc", k_) for k_ in range(NT // 16)]
                    f.op(f.act, lambda a: a.activation(out=Rr[:], in_=Nacc[64:128, :], func=AF.Ln), reads=nkeys, writes=["Rb"])
                    f.op(f.act, lambda a: a.activation(out=Rr[:], in_=Rr[:], func=AF.Exp, scale=-1.0), reads=["Rb"], writes=["Rb"])
                    f.op(f.dve, lambda v: v.tensor_tensor(out=Oh[s][:], in0=Nacc[0:64, :], in1=Rr[:], op=ALU.mult),
                         reads=nkeys + ["Rb"], writes=[f"Oh{s}"])
                    f.dma(f.sp, f"bo{s}", OTB[hh * 64:(hh + 1) * 64, :], Oh[s][:], reads=[f"Oh{s}"], writes=[("OTB", hh)])
                f.barrier()

        pass_A1()
        load_w(Wdn0, "Wdn", ffn_w_down[0], NJ, D, sem="wld")
        pass_A2()
        esA.close()
        pass_OF(0, OT, 8, a_w_out, x_in, X1, 2, H1T, X2, False, Wdn_pre=Wdn0)
        esW0.close()
        pass_B1()
        pass_B2()
        pass_OF(1, OTB, 4, b_w_out, X2, X3, 3, H3T, None, True)
        f.barrier()
    return nc


def _consts():
    ident = np.eye(128, dtype=np.float32)
    s = np.arange(128)[:, None]
    t = np.arange(128)[None, :]
    tri = (t >= s).astype(np.float32)
    utri = (s <= t).astype(np.float32)
    ones = np.ones((128, 128), np.float32)
    et = np.zeros((24, 128, 256), np.float64)
    j = np.arange(128)[:, None].astype(np.float64)
    i = np.arange(128)[None, :].astype(np.float64)
    for g, (_, dil) in enumerate(B_GROUPS):
        for hh in range(8):
            k = g * 8 + hh + 1
            slope = np.float64(np.float32(2.0) ** np.float32(-8.0 * k / 24.0))
            sig = slope * dil
            cur = np.where(i >= j, np.exp(-sig * (i - j)), 0.0)
            prv = np.where(i <= j, np.exp(-sig * (i + 128 - j)), 0.0)
            et[g * 8 + hh, :, 0:128] = cur
            et[g * 8 + hh, :, 128:256] = prv
    return {
        "c_ident": ident.astype(ml_dtypes.bfloat16),
        "c_tri": tri.astype(ml_dtypes.bfloat16),
        "c_utri": utri,
        "c_ones": ones,
        "c_etab": et.astype(np.float32),
    }


def _common_inputs(inp):
    f32 = lambda a: np.ascontiguousarray(np.asarray(a, dtype=np.float32))
    gains = np.zeros((7, D), np.float32)
    gains[0:2] = f32(inp["mix_norm_g"])
    gains[2:4] = f32(inp["ffn_norm_g"])
    gains[4] = f32(inp["kv_norm_g"])
    gains[5] = f32(inp["final_norm_g"])
    cw = f32(inp["ffn_conv_w"])
    cb = f32(inp["ffn_conv_b"])
    cwb = np.concatenate([cw, cb[:, None, :]], axis=1)
    cwb = np.ascontiguousarray(cwb.reshape(2, 4, NCH, 128).transpose(0, 3, 2, 1))
    d = {
        "a_w_in": f32(inp["a_w_in"])[0], "a_b_f": f32(inp["a_b_f"]).reshape(1, 16), "a_w_out": f32(inp["a_w_out"])[0],
        "b_w_q": f32(inp["b_w_q"])[0], "b_w_out": f32(inp["b_w_out"])[0], "w_kv": f32(inp["w_kv"]),
        "gains": gains, "ffn_w_up": f32(inp["ffn_w_up"]), "ffn_w_down": f32(inp["ffn_w_down"]), "cwb": cwb,
    }
    d.update(_consts())
    return d


def kernel(**inputs):
    x = np.asarray(inputs["x"], dtype=np.float32)
    B, S, _ = x.shape
    nc = build(S)
    common = _common_inputs(inputs)
    in_maps = [dict(common, x=np.ascontiguousarray(x[b])) for b in range(B)]
    res = run_bass_kernel_spmd(nc, in_maps, core_ids=list(range(B)))
    return np.stack([np.asarray(r["out"], dtype=np.float32) for r in res.results], axis=0)
```

```python
import contextlib
import numpy as np
import ml_dtypes
import concourse.bass as bass
import concourse.mybir as mybir
from concourse.bass_utils import run_bass_kernel_spmd

F32 = mybir.dt.float32
BF16 = mybir.dt.bfloat16
AF = mybir.ActivationFunctionType
ALU = mybir.AluOpType

D = 1024
NH_A = 16
DFF = 2816
NCH = 44
NJ = 22
EPS = 1e-6
B_GROUPS = ((128, 1), (512, 4), (2048, 16))


class T:
    __slots__ = ("sem", "val")

    def __init__(self, sem, val):
        self.sem = sem
        self.val = val


class Eng:
    def __init__(self, name, b, sem):
        self.name = name
        self.b = b
        self.sem = sem
        self.count = 0
        self.waited = {}

    def wait(self, tickets):
        for t in tickets:
            sem, val = t.sem, t.val
            if val <= 0:
                continue
            if self.waited.get(sem, 0) < val:
                self.b.wait_ge(sem, val)
                self.waited[sem] = val


class FW:
    def __init__(self, nc, sems):
        self.nc = nc
        self.sems = list(sems)
        self.pe = Eng("pe", nc.tensor, self.sems.pop())
        self.act = Eng("act", nc.scalar, self.sems.pop())
        self.dve = Eng("dve", nc.vector, self.sems.pop())
        self.pool = Eng("pool", nc.gpsimd, self.sems.pop())
        self.sp = Eng("sp", nc.sync, None)
        self.engs = [self.pe, self.act, self.dve, self.pool, self.sp]
        self.lastw = {}
        self.readers = {}
        self.dmasem = {}

    def _deps(self, reads, writes):
        t = []
        for k in reads:
            if k in self.lastw:
                t.append(self.lastw[k])
        for k in writes:
            if k in self.lastw:
                t.append(self.lastw[k])
            t += list(self.readers.get(k, {}).values())
        return t

    def _commit(self, ticket, reads, writes):
        for k in writes:
            self.lastw[k] = ticket
            self.readers[k] = {}
        for k in reads:
            d = self.readers.setdefault(k, {})
            if ticket.sem not in d or d[ticket.sem].val < ticket.val:
                d[ticket.sem] = ticket

    def op(self, eng, fn, reads=(), writes=()):
        eng.wait(self._deps(reads, writes))
        ins = fn(eng.b)
        eng.count += 1
        ins.then_inc(eng.sem, 1)
        self._commit(T(eng.sem, eng.count), reads, writes)

    def mm(self, fn, reads=(), writes=(), last=False):
        eng = self.pe
        deps = [t for t in self._deps(reads, writes) if t.sem is not eng.sem]
        eng.wait(deps)
        ins = fn(eng.b)
        self._commit(T(eng.sem, eng.count + 1), reads, writes)
        if last:
            eng.count += 1
            ins.then_inc(eng.sem, 1)

    def dma(self, eng, semname, out, in_, reads=(), writes=(), **kw):
        if semname not in self.dmasem:
            self.dmasem[semname] = [self.sems.pop(), 0, []]
        ent = self.dmasem[semname]
        eng.wait(self._deps(reads, writes))
        ins = eng.b.dma_start(out=out, in_=in_, **kw)
        if eng.waited.get(ent[0], 0) >= ent[1]:
            ent[2] = []
        ent[1] += 16
        ins.then_inc(ent[0], 16)
        for t in ent[2]:
            t.val = ent[1]
        tk = T(ent[0], ent[1])
        ent[2].append(tk)
        self._commit(tk, reads, writes)

    def barrier(self, keep=None):
        keep = keep or (lambda k: False)
        t = [v for k, v in self.lastw.items() if not keep(k)]
        for k, d in self.readers.items():
            if not keep(k):
                t += list(d.values())
        for e in self.engs:
            e.wait(t)
        self.lastw = {k: v for k, v in self.lastw.items() if keep(k)}
        self.readers = {k: v for k, v in self.readers.items() if keep(k)}


def build(S):
    NT = S // 128
    NB = S // 512
    nc = bass.Bass("TRN2", target_bir_lowering=False)

    def din(name, shape, dt=F32):
        return nc.dram_tensor(name, list(shape), dt, kind="ExternalInput").ap()

    def dscr(name, shape, dt):
        return nc.dram_tensor(name, list(shape), dt, kind="Internal").ap()

    x_in = din("x", [S, D])
    a_w_in = din("a_w_in", [D, 3088])
    a_b_f = din("a_b_f", [1, 16])
    a_w_out = din("a_w_out", [D, D])
    b_w_q = din("b_w_q", [D, 1536])
    b_w_out = din("b_w_out", [512, D])
    w_kv = din("w_kv", [D, 3072])
    gains = din("gains", [7, D])
    ffn_w_up = din("ffn_w_up", [2, D, 2 * DFF])
    ffn_w_down = din("ffn_w_down", [2, DFF, D])
    cwb = din("cwb", [2, 128, NCH, 4])
    c_ident = din("c_ident", [128, 128], BF16)
    c_tri = din("c_tri", [128, 128], BF16)
    c_utri = din("c_utri", [128, 128])
    c_ones = din("c_ones", [128, 128])
    c_etab = din("c_etab", [24, 128, 256])
    out = nc.dram_tensor("out", [S, D], F32, kind="ExternalOutput").ap()

    QT = dscr("QT", [D, S], BF16)
    KT = dscr("KT", [D, S], BF16)
    VV = dscr("VV", [S, D], BF16)
    OT = dscr("OT", [D, S], BF16)
    X1 = dscr("X1", [S, D], F32)
    H1T = dscr("H1T", [D, S], BF16)
    X2 = dscr("X2", [S, D], F32)
    QTB = dscr("QTB", [1536, S], BF16)
    KTB = dscr("KTB", [1536, S], BF16)
    VB = dscr("VB", [3, S, 512], BF16)
    OTB = dscr("OTB", [512, S], BF16)
    X3 = dscr("X3", [S, D], F32)
    H3T = dscr("H3T", [D, S], BF16)

    es0 = contextlib.ExitStack()
    with es0:
        sems = [es0.enter_context(nc.semaphore(f"s{i}")) for i in range(100)]
        f = FW(nc, sems)
        PB = [es0.enter_context(nc.psum_tensor(f"pb{i}", [128, 512], F32)) for i in range(6)]
        PTB = [es0.enter_context(nc.psum_tensor(f"ptb{i}", [128, 1024], BF16)) for i in range(2)]

        uniq = [0]

        def sbt(es, name, shape, dt):
            uniq[0] += 1
            return es.enter_context(nc.sbuf_tensor(f"{name}_u{uniq[0]}", list(shape), dt))

        ident = sbt(es0, "ident", [128, 128], BF16)
        tri = sbt(es0, "tri", [128, 128], BF16)
        negh = sbt(es0, "negh", [128, 1], F32)
        esW0 = contextlib.ExitStack()
        Wdn0 = sbt(esW0, "Wdn0", [128, NJ, D], BF16)
        keepw_g = lambda k: k == "cw" or (isinstance(k, tuple) and k[0] in ("Wup", "Wdn"))
        esA = contextlib.ExitStack()
        BT = sbt(esA, "BT", [128, 16, NB, NT], F32)
        f.dma(f.sp, "c0", ident[:], c_ident[:, :], writes=["ident"])
        f.dma(f.sp, "c0", tri[:], c_tri[:, :], writes=["tri"])
        f.op(f.pool, lambda g: g.memset(negh[:], -0.5), writes=["negh"])

        def load_w(Wt, key, src, KC, N, CH=2048, sem="wl"):
            for kc in range(KC):
                for c0 in range(0, N, CH):
                    c1 = min(N, c0 + CH)
                    f.dma(f.pool, sem, Wt[:, kc, c0:c1], src[kc * 128:(kc + 1) * 128, c0:c1],
                          writes=[(key, kc)])

        def load_gain(Gt, key, row):
            f.dma(f.sp, "c0", Gt[:], gains[row:row + 1, :].partition_broadcast(128), writes=[key])

        def rstd_of(es_tmp, xt_ap, xkey, tmp):
            junk, ss, rs = tmp["junk"], tmp["ss"], tmp["rs"]
            f.op(f.dve, lambda v: v.scalar_tensor_tensor(out=junk[:], in0=xt_ap, scalar=1.0, in1=xt_ap,
                                                        op0=ALU.mult, op1=ALU.mult, accum_out=ss[:, 0:1]),
                 reads=[xkey], writes=["junk", "ss"])
            f.op(f.act, lambda a: a.activation(out=ss[:, 1:2], in_=ss[:, 0:1], func=AF.Sqrt, bias=EPS, scale=1.0 / D),
                 reads=["ss"], writes=["ss2"])
            f.op(f.dve, lambda v: v.reciprocal(rs[:], ss[:, 1:2]), reads=["ss2"], writes=["rs"])

        def mk_norm_tmp(es):
            return {"junk": sbt(es, "junk", [128, D], BF16), "ss": sbt(es, "ss", [128, 2], F32),
                    "rs": sbt(es, "rs", [128, 1], F32)}

        tcount = [0]

        def norm_T_store(xt_ap, xkey, tmp, Gt, gkey, hb, hbkey, hTs, hTskey, HTdst, tok0, semname):
            rs = tmp["rs"]
            f.op(f.dve, lambda v: v.scalar_tensor_tensor(out=hb[:], in0=xt_ap, scalar=rs[:, 0:1], in1=Gt[:],
                                                        op0=ALU.mult, op1=ALU.mult),
                 reads=[xkey, "rs", gkey], writes=[hbkey])
            pi = tcount[0] % 2
            tcount[0] += 1
            ptb = PTB[pi]
            for kc in range(8):
                f.mm(lambda t, kc=kc: t.transpose(ptb[:, kc * 128:(kc + 1) * 128], hb[:, kc * 128:(kc + 1) * 128], ident[:]),
                     reads=[hbkey, "ident"], writes=[f"PT{pi}"], last=(kc == 7))
            f.op(f.act, lambda a: a.copy(hTs[:].rearrange("p k t -> p (k t)"), ptb[:]), writes=[hTskey, f"PT{pi}"])
            f.dma(f.sp, semname, HTdst.rearrange("(k p) s -> p k s", p=128)[:, :, tok0:tok0 + 128], hTs[:],
                  reads=[hTskey], writes=[("HTdst", id(HTdst), tok0)])

        def pass_A1():
            with contextlib.ExitStack() as es:
                Win = sbt(es, "Win", [128, 8, 3088], BF16)
                G = sbt(es, "G0", [128, D], F32)
                Bf = sbt(es, "Bf", [128, 16], F32)
                utri = sbt(es, "utri", [128, 128], F32)
                ones = sbt(es, "ones", [128, 128], F32)
                xt = [sbt(es, f"xt{i}", [128, D], F32) for i in range(2)]
                hb = [sbt(es, f"hb{i}", [128, D], BF16) for i in range(2)]
                hTb = [sbt(es, f"hTb{i}", [128, 8, 512], BF16) for i in range(2)]
                qst = [sbt(es, f"qst{i}", [128, 512], BF16) for i in range(3)]
                vst = [sbt(es, f"vst{i}", [128, D], BF16) for i in range(2)]
                zall = sbt(es, "zall", [128, NT, 16], F32)
                dall = sbt(es, "dall", [128, NT, 16], F32)
                CW = sbt(es, "CW", [128, NT, 32], F32)
                carry = sbt(es, "carry", [128, NT + 1, 16], F32)
                Dfull = sbt(es, "Dfull", [128, NT, 16], F32)
                tmp = mk_norm_tmp(es)
                load_gain(G, "G", 0)
                f.dma(f.sp, "c0", Bf[:], a_b_f[0:1, :].partition_broadcast(128), writes=["Bf"])
                f.dma(f.sp, "c0", utri[:], c_utri[:, :], writes=["utri"])
                f.dma(f.sp, "c0", ones[:], c_ones[:, :], writes=["ones"])
                load_w(Win, "Win", a_w_in, 8, 3088)
                wkeys = [("Win", kc) for kc in range(8)]
                bank = [0]
                ev = [0]

                def evac(dst_ap, dkey, pbi, src_ap):
                    e = f.act if ev[0] % 2 == 0 else f.dve
                    ev[0] += 1
                    if e is f.act:
                        f.op(e, lambda a: a.copy(dst_ap, src_ap), writes=[dkey, f"PB{pbi}"])
                    else:
                        f.op(e, lambda v: v.tensor_copy(dst_ap, src_ap), writes=[dkey, f"PB{pbi}"])

                qs = [0]

                def ld_x(n):
                    f.dma(f.sp, f"xl{n % 2}", xt[n % 2][:], x_in[n * 128:(n + 1) * 128, :], writes=[f"xt{n % 2}"])

                ld_x(0)

                def norm_chain(n):
                    xs = xt[n % 2]
                    xk = f"xt{n % 2}"
                    if n + 1 < NT:
                        ld_x(n + 1)
                    rstd_of(es, xs[:], xk, tmp)
                    h = hb[n % 2]
                    hbk = f"hb{n % 2}"
                    rs = tmp["rs"]
                    f.op(f.dve, lambda v: v.scalar_tensor_tensor(out=h[:], in0=xs[:], scalar=rs[:, 0:1], in1=G[:],
                                                                op0=ALU.mult, op1=ALU.mult),
                         reads=[xk, "rs", "G"], writes=[hbk])

                def norm_trans(n):
                    b_, tl_ = n // 4, n % 4
                    hT_ = hTb[b_ % 2]
                    hk_ = f"hTb{b_ % 2}"
                    h = hb[n % 2]
                    hbk = f"hb{n % 2}"
                    pi = n % 2
                    ptb = PTB[pi]
                    for kc in range(8):
                        f.mm(lambda t, kc=kc: t.transpose(ptb[:, kc * 128:(kc + 1) * 128], h[:, kc * 128:(kc + 1) * 128], ident[:]),
                             reads=[hbk, "ident"], writes=[f"PT{pi}"], last=(kc == 7))
                    f.op(f.act, lambda a: a.copy(hT_[:, :, tl_ * 128:(tl_ + 1) * 128],
                                                 ptb[:].rearrange("p (k t) -> p k t", k=8)),
                         writes=[(hk_, tl_), f"PT{pi}"])

                def norm_tile(n):
                    norm_chain(n)
                    norm_trans(n)

                for tl in range(4):
                    norm_tile(tl)
                for b in range(NB):
                    hT = hTb[b % 2]
                    hk = f"hTb{b % 2}"
                    hkeys = [(hk, tl) for tl in range(4)]
                    for j in range(16):
                        pbi = bank[0] % 4
                        bank[0] += 1
                        for kc in range(8):
                            f.mm(lambda t, kc=kc, j=j, pbi=pbi: t.matmul(PB[pbi][:], lhsT=Win[:, kc, j * 128:(j + 1) * 128], rhs=hT[:, kc, :],
                                                                       start=(kc == 0), stop=(kc == 7)),
                                 reads=[wkeys[kc]] + hkeys, writes=[f"PB{pbi}"], last=(kc == 7))
                        st = qst[qs[0] % 3]
                        sk = f"qst{qs[0] % 3}"
                        qs[0] += 1
                        evac(st[:], sk, pbi, PB[pbi][:])
                        dst = QT if j < 8 else KT
                        jj = j % 8
                        f.dma(f.sp, sk, dst[jj * 128:(jj + 1) * 128, b * 512:(b + 1) * 512], st[:], reads=[sk],
                              writes=[("qk", j, b)])
                        if j % 4 == 0 and b + 1 < NB:
                            norm_chain((b + 1) * 4 + j // 4)
                        if j % 4 == 3 and b + 1 < NB:
                            norm_trans((b + 1) * 4 + j // 4)
                    for tl in range(4):
                        n = b * 4 + tl
                        vs = vst[n % 2]
                        vk = f"vst{n % 2}"
                        for half in range(2):
                            pbi = bank[0] % 4
                            bank[0] += 1
                            for kc in range(8):
                                f.mm(lambda t, kc=kc, half=half, pbi=pbi: t.matmul(PB[pbi][:], lhsT=hT[:, kc, tl * 128:(tl + 1) * 128],
                                                                                 rhs=Win[:, kc, 2048 + half * 512:2048 + (half + 1) * 512],
                                                                                 start=(kc == 0), stop=(kc == 7)),
                                     reads=[wkeys[kc], (hk, tl)], writes=[f"PB{pbi}"], last=(kc == 7))
                            evac(vs[:, half * 512:(half + 1) * 512], (vk, half), pbi, PB[pbi][:])
                        f.dma(f.sp, vk, VV[n * 128:(n + 1) * 128, :], vs[:], reads=[(vk, 0), (vk, 1)], writes=[("vv", n)])
                        for kc in range(8):
                            f.mm(lambda t, kc=kc: t.matmul(PB[4][:, 0:16], lhsT=hT[:, kc, tl * 128:(tl + 1) * 128], rhs=Win[:, kc, 3072:3088],
                                                           start=(kc == 0), stop=(kc == 7)),
                                 reads=[wkeys[kc], (hk, tl)], writes=["PB4"], last=(kc == 7))
                        f.op(f.dve, lambda v, n=n: v.tensor_tensor(out=zall[:, n, :], in0=PB[4][:, 0:16], in1=Bf[:], op=ALU.add),
                             reads=["Bf"], writes=[("zall", n), "PB4"])
                zk = [("zall", n) for n in range(NT)]
                f.op(f.act, lambda a: a.activation(out=zall[:], in_=zall[:], func=AF.Exp, scale=-1.0), reads=zk, writes=zk)
                f.op(f.act, lambda a: a.activation(out=dall[:], in_=zall[:], func=AF.Ln, bias=1.0, scale=1.0),
                     reads=zk, writes=[("dall", n) for n in range(NT)])
                for n in range(NT):
                    pb = PB[n // 16]
                    o = (n % 16) * 32
                    f.mm(lambda t, n=n, pb=pb, o=o: t.matmul(pb[:, o:o + 16], lhsT=utri[:], rhs=dall[:, n, :], start=True, stop=True),
                         reads=["utri", ("dall", n)], writes=[f"PB{n // 16}"], last=False)
                    f.mm(lambda t, n=n, pb=pb, o=o: t.matmul(pb[:, o + 16:o + 32], lhsT=ones[:], rhs=dall[:, n, :], start=True, stop=True),
                         reads=["ones", ("dall", n)], writes=[f"PB{n // 16}"], last=True)
                for hb_ in range((NT + 15) // 16):
                    n0 = hb_ * 16
                    n1 = min(NT, n0 + 16)
                    f.op(f.dve, lambda v, n0=n0, n1=n1, hb_=hb_: v.tensor_copy(CW[:, n0:n1, :].rearrange("p n c -> p (n c)"),
                                                                            PB[hb_][:, 0:(n1 - n0) * 32]),
                         writes=["CW", f"PB{hb_}"])
                f.op(f.dve, lambda v: v.memset(carry[:, 0, :], 0.0), writes=[("carry", 0)])
                for n in range(NT):
                    f.op(f.dve, lambda v, n=n: v.tensor_tensor(out=carry[:, n + 1, :], in0=carry[:, n, :], in1=CW[:, n, 16:32], op=ALU.add),
                         reads=[("carry", n), "CW"], writes=[("carry", n + 1)])
                f.op(f.dve, lambda v: v.tensor_tensor(out=Dfull[:], in0=CW[:, :, 0:16], in1=carry[:, 0:NT, :], op=ALU.add),
                     reads=["CW"] + [("carry", n) for n in range(NT + 1)], writes=["Dfull"])
                for h in range(16):
                    for Jb in range(NB):
                        f.op(f.dve, lambda v, h=h, Jb=Jb: v.tensor_scalar(out=BT[:, h, Jb, :], in0=Dfull[:, :, h],
                                                                        scalar1=carry[:, 4 * Jb + 2, h:h + 1], scalar2=None,
                                                                        op0=ALU.subtract),
                             reads=["Dfull", ("carry", 4 * Jb + 2)], writes=["BT"])
                f.barrier()

        def pass_A2():
            with contextlib.ExitStack() as es:
                KTp = [sbt(es, f"KTp{i}", [128, S], BF16) for i in range(2)]
                QTz = [[sbt(es, f"QTz{i}_{a}", [128, S], BF16) for a in range(2)] for i in range(2)]
                Va = [[sbt(es, f"Va{i}_{a}", [128, NT, 128], BF16) for a in range(2)] for i in range(2)]
                OTh = [[sbt(es, f"OTh{i}_{a}", [64, S], BF16) for a in range(2)] for i in range(2)]
                PTt = [sbt(es, f"PTt{i}", [128, 512], BF16) for i in range(6)]
                Rr = [sbt(es, f"Rr{i}", [64, 512], F32) for i in range(2)]
                for i in range(2):
                    for a in range(2):
                        f.op(f.pool, lambda g, i=i, a=a: g.memset(Va[i][a][:, :, 64:128], 1.0), writes=[(f"Va{i}_{a}", "ones")])
                        z0 = 64 if a == 0 else 0
                        f.op(f.pool, lambda g, i=i, a=a, z0=z0: g.memset(QTz[i][a][z0:z0 + 64, :], 0.0), writes=[(f"QTz{i}_{a}", "z")])
                VVr = VV.rearrange("(n p) c -> p n c", p=128)

                def load_pair(hp):
                    s = hp % 2
                    f.dma(f.sp, f"hk{s}", KTp[s][:], KT[hp * 128:(hp + 1) * 128, :], writes=[f"KTp{s}"])
                    for a in range(2):
                        h = 2 * hp + a
                        q0 = 0 if a == 0 else 64
                        f.dma(f.sp, f"hq{s}", QTz[s][a][q0:q0 + 64, :], QT[h * 64:(h + 1) * 64, :], writes=[(f"QTz{s}_{a}", "q")])
                        f.dma(f.sp, f"hv{s}", Va[s][a][:, :, 0:64], VVr[:, :, h * 64:(h + 1) * 64], writes=[(f"Va{s}_{a}", "v")])

                load_pair(0)
                ucount = [0]
                oac = [0]
                LA = 3
                for hp in range(NH_A // 2):
                    if hp + 1 < NH_A // 2:
                        load_pair(hp + 1)
                    s = hp % 2
                    for a in range(2):
                        h = 2 * hp + a
                        units = [(Jb, i) for Jb in range(NB) for i in range(4 * Jb + 4)]
                        info = {}
                        Qz, Vh, Oh_ = QTz[s][a], Va[s][a], OTh[s][a]
                        qkeys = [(f"QTz{s}_{a}", "q"), (f"QTz{s}_{a}", "z"), f"KTp{s}"]
                        vkeys = [(f"Va{s}_{a}", "v"), (f"Va{s}_{a}", "ones")]
                        okey = f"OTh{s}_{a}"

                        def emit_qk(u):
                            Jb, i = u
                            il = i - 4 * Jb
                            c0 = 128 * il if il > 0 else 0
                            uid = ucount[0]
                            ucount[0] += 1
                            sb_ = uid % 4
                            pt = uid % 6
                            info[u] = (c0, pt)
                            f.mm(lambda t: t.matmul(PB[sb_][:, c0:512], lhsT=KTp[s][:, i * 128:(i + 1) * 128],
                                                    rhs=Qz[:, Jb * 512 + c0:(Jb + 1) * 512], start=True, stop=True),
                                 reads=qkeys, writes=[f"PB{sb_}"], last=True)
                            f.op(f.act, lambda a_: a_.activation(out=PTt[pt][:, c0:512], in_=PB[sb_][:, c0:512], func=AF.Exp,
                                                                 bias=BT[:, h, Jb, i:i + 1], scale=0.125),
                                 reads=["BT"], writes=[f"PTt{pt}", f"PB{sb_}"])
                            if il >= 0:
                                f.op(f.dve, lambda g: g.tensor_tensor(out=PTt[pt][:, c0:c0 + 128], in0=PTt[pt][:, c0:c0 + 128], in1=tri[:],
                                                                       op=ALU.mult),
                                     reads=["tri"], writes=[f"PTt{pt}"])

                        def emit_pv(u):
                            Jb, i = u
                            c0, pt = info[u]
                            ob = 4 + (oac[0] % 2)
                            lasti = (i == 4 * Jb + 3)
                            f.mm(lambda t: t.matmul(PB[ob][:, c0:512], lhsT=Vh[:, i, :], rhs=PTt[pt][:, c0:512],
                                                    start=(i == 0), stop=lasti),
                                 reads=vkeys + [f"PTt{pt}"], writes=[f"PB{ob}"], last=lasti)
                            if lasti:
                                r = Rr[oac[0] % 2]
                                rk = f"Rr{oac[0] % 2}"
                                f.op(f.dve, lambda v: v.reciprocal(r[:], PB[ob][64:128, :]), writes=[rk, f"PB{ob}"])
                                f.op(f.dve, lambda v: v.tensor_tensor(out=Oh_[:, Jb * 512:(Jb + 1) * 512], in0=PB[ob][0:64, :], in1=r[:],
                                                                      op=ALU.mult),
                                     reads=[rk], writes=[(okey, Jb), f"PB{ob}"])
                                oac[0] += 1

                        for idx in range(len(units) + LA):
                            if idx < len(units):
                                emit_qk(units[idx])
                            if idx - LA >= 0:
                                emit_pv(units[idx - LA])
                        f.dma(f.sp, f"ho{s}", OT[h * 64:(h + 1) * 64, :], Oh_[:], reads=[(okey, Jb) for Jb in range(NB)],
                              writes=[("OT", h)])
                f.barrier(keep=keepw_g)

        def pass_OF(layer, OTsrc, KC, Wsrc, Xin, Xmid, grow, HT, Xout, final, Wdn_pre=None):
            with contextlib.ExitStack() as esw:
                Wup = sbt(esw, "Wup", [128, 8, 2 * DFF], BF16)
                Wdn = Wdn_pre if Wdn_pre is not None else sbt(esw, "Wdn", [128, NJ, D], BF16)
                cw = sbt(esw, "cw", [128, NCH, 4], F32)
                keepw = lambda k: k == "cw" or (isinstance(k, tuple) and k[0] in ("Wup", "Wdn"))
                with contextlib.ExitStack() as es:
                    W = sbt(es, "Wo", [128, KC, D], BF16)
                    G = sbt(es, "Go", [128, D], F32)
                    ob = [sbt(es, f"ob{i}", [128, KC, 512], BF16) for i in range(2)]
                    xt = [sbt(es, f"xt{i}", [128, D], F32) for i in range(3)]
                    xn = [sbt(es, f"xn{i}", [128, D], F32) for i in range(3)]
                    hb = [sbt(es, f"hb{i}", [128, D], BF16) for i in range(2)]
                    hTs = [sbt(es, f"hTs{i}", [128, 8, 128], BF16) for i in range(2)]
                    tmp = mk_norm_tmp(es)
                    load_gain(G, "G", grow)
                    load_w(W, "W", Wsrc, KC, D)
                    f.dma(f.sp, "c0", cw[:], cwb[layer], writes=["cw"])
                    load_w(Wup, "Wup", ffn_w_up[layer], 8, 2 * DFF, sem="wlu")
                    if Wdn_pre is None:
                        load_w(Wdn, "Wdn", ffn_w_down[layer], NJ, D, sem="wld")
                    OTr = OTsrc.rearrange("(k p) s -> p k s", p=128)
                    bank = [0]

                    def ld_ob(b):
                        f.dma(f.sp, f"ol{b % 2}", ob[b % 2][:], OTr[:, :, b * 512:(b + 1) * 512], writes=[f"ob{b % 2}"])

                    def ld_x(n):
                        f.dma(f.sp, f"xl{n % 3}", xt[n % 3][:], Xin[n * 128:(n + 1) * 128, :], writes=[f"xt{n % 3}"])

                    ld_ob(0)
                    ld_x(0)

                    def mm_tile(n):
                        b, tl = n // 4, n % 4
                        o = ob[b % 2]
                        ok = f"ob{b % 2}"
                        if tl == 0 and b + 1 < NB:
                            ld_ob(b + 1)
                        xs, xk = xt[n % 3], f"xt{n % 3}"
                        if n + 1 < NT:
                            ld_x(n + 1)
                        xo, xok = xn[n % 3], f"xn{n % 3}"
                        for half in range(2):
                            pbi = bank[0] % 4
                            bank[0] += 1
                            for kc in range(KC):
                                f.mm(lambda t, kc=kc, half=half, pbi=pbi: t.matmul(PB[pbi][:], lhsT=o[:, kc, tl * 128:(tl + 1) * 128],
                                                                                 rhs=W[:, kc, half * 512:(half + 1) * 512],
                                                                                 start=(kc == 0), stop=(kc == KC - 1)),
                                     reads=[("W", kc), ok], writes=[f"PB{pbi}"], last=(kc == KC - 1))
                            f.op(f.dve, lambda v, half=half, pbi=pbi: v.tensor_tensor(out=xo[:, half * 512:(half + 1) * 512], in0=PB[pbi][:],
                                                                                    in1=xs[:, half * 512:(half + 1) * 512], op=ALU.add),
                                 reads=[xk], writes=[(xok, half), f"PB{pbi}"])
                        f.dma(f.sp, f"xs{n % 3}", Xmid[n * 128:(n + 1) * 128, :], xo[:], reads=[(xok, 0), (xok, 1)], writes=[("Xmid", n)])

                    def norm_chain(n):
                        xo, xok = xn[n % 3], f"xn{n % 3}"
                        junk, ss, rs = tmp["junk"], tmp["ss"], tmp["rs"]
                        f.op(f.act, lambda a: a.activation(out=junk[:], in_=xo[:], func=AF.Square, accum_out=ss[:, 0:1]),
                             reads=[(xok, 0), (xok, 1)], writes=["junk", "ss"])
                        f.op(f.act, lambda a: a.activation(out=ss[:, 1:2], in_=ss[:, 0:1], func=AF.Sqrt, bias=EPS, scale=1.0 / D),
                             reads=["ss"], writes=["ss2"])
                        f.op(f.dve, lambda v: v.reciprocal(rs[:], ss[:, 1:2]), reads=["ss2"], writes=["rs"])
                        h, hbk = hb[n % 2], f"hb{n % 2}"
                        f.op(f.dve, lambda v: v.scalar_tensor_tensor(out=h[:], in0=xo[:], scalar=rs[:, 0:1], in1=G[:], op0=ALU.mult,
                                                                    op1=ALU.mult),
                             reads=[(xok, 0), (xok, 1), "rs", "G"], writes=[hbk])

                    def norm_trans(n):
                        h, hbk = hb[n % 2], f"hb{n % 2}"
                        pi = n % 2
                        ptb = PTB[pi]
                        for kc in range(8):
                            f.mm(lambda t, kc=kc: t.transpose(ptb[:, kc * 128:(kc + 1) * 128], h[:, kc * 128:(kc + 1) * 128], ident[:]),
                                 reads=[hbk, "ident"], writes=[f"PT{pi}"], last=(kc == 7))
                        hs, hsk = hTs[n % 2], f"hTs{n % 2}"
                        f.op(f.act, lambda a: a.copy(hs[:].rearrange("p k t -> p (k t)"), ptb[:]), writes=[hsk, f"PT{pi}"])
                        f.dma(f.sp, f"hs{n % 2}", HT.rearrange("(k p) s -> p k s", p=128)[:, :, n * 128:(n + 1) * 128], hs[:],
                              reads=[hsk], writes=[("HT", n)])

                    mm_tile(0)
                    mm_tile(1)
                    for n in range(NT):
                        norm_chain(n)
                        if n + 2 < NT:
                            mm_tile(n + 2)
                        norm_trans(n)
                    f.barrier(keep=keepw)
                with contextlib.ExitStack() as es:
                    MT = sbt(es, "MT", [128, NJ, 512], BF16)
                    NHB = 2
                    hTb = [sbt(es, f"hTf{i}", [128, 8, 512], BF16) for i in range(NHB)]
                    ub = [sbt(es, f"ub{i}", [128, 516], F32) for i in range(3)]
                    tails = sbt(es, "tails", [128, NCH, 2], F32)
                    cv = [[sbt(es, f"cv{w_}_{i}", [128, 512], F32) for i in range(3)] for w_ in range(2)]
                    sg = [sbt(es, f"sg{i}", [128, 512], F32) for i in range(2)]
                    xt = [sbt(es, f"xtf{i}", [128, D], F32) for i in range(2)]
                    if final:
                        G = sbt(es, "Gf", [128, D], F32)
                        tmp = mk_norm_tmp(es)
                        load_gain(G, "G", 5)
                    f.op(f.dve, lambda v: v.memset(tails[:], 0.0), writes=[("tails", c) for c in range(NCH)])
                    HTr = HT.rearrange("(k p) s -> p k s", p=128)
                    ubank = [0]
                    pbank = [0]

                    def ld_h(b):
                        f.dma(f.sp, f"hl{b % NHB}", hTb[b % NHB][:], HTr[:, :, b * 512:(b + 1) * 512], writes=[f"hT{b % NHB}"])

                    def ld_x(n):
                        f.dma(f.sp, f"xl{n % 2}", xt[n % 2][:], Xmid[n * 128:(n + 1) * 128, :], writes=[f"xt{n % 2}"])

                    ld_h(0)
                    ld_x(0)

                    def gate_mult(j, cvs):
                        sgt, sgk = sg[j % 2], f"sg{j % 2}"
                        f.op(f.act, lambda a: a.activation(out=sgt[:], in_=cvs[1][0][:], func=AF.Silu), reads=[cvs[1][1]], writes=[sgk])
                        me = f.dve if j % 2 == 0 else f.pool
                        f.op(me, lambda g: g.tensor_tensor(out=MT[:, j, :], in0=sgt[:], in1=cvs[0][0][:], op=ALU.mult),
                             reads=[sgk, cvs[0][1]], writes=[("MT", j)])

                    def up_pair(b, j):
                        hT, hk = hTb[b % NHB], f"hT{b % NHB}"
                        cvs = []
                        for which in range(2):
                            c = j + which * NJ
                            pbi = pbank[0] % 6
                            pbank[0] += 1
                            u = ub[ubank[0] % 3]
                            uk = f"ub{ubank[0] % 3}"
                            ubank[0] += 1
                            for kc in range(8):
                                f.mm(lambda t, kc=kc, c=c, pbi=pbi: t.matmul(PB[pbi][:], lhsT=Wup[:, kc, c * 128:(c + 1) * 128], rhs=hT[:, kc, :],
                                                                          start=(kc == 0), stop=(kc == 7)),
                                     reads=[("Wup", kc), hk], writes=[f"PB{pbi}"], last=(kc == 7))
                            f.op(f.act, lambda a, u=u, pbi=pbi: a.copy(u[:, 2:514], PB[pbi][:]), writes=[(uk, "b"), f"PB{pbi}"])
                            f.op(f.dve, lambda g, c=c, u=u: g.tensor_copy(u[:, 0:2], tails[:, c, :]), reads=[("tails", c)], writes=[(uk, "h")])
                            f.op(f.dve, lambda g, c=c, u=u: g.tensor_copy(tails[:, c, :], u[:, 512:514]), reads=[(uk, "b")], writes=[("tails", c)])
                            cvt = cv[which][j % 3]
                            ck = f"cv{which}_{j % 3}"
                            cvs.append((cvt, ck))
                            f.op(f.act, lambda a, c=c, u=u, cvt=cvt: a.activation(out=cvt[:], in_=u[:, 2:514], func=AF.Identity,
                                                                               bias=cw[:, c, 3:4], scale=cw[:, c, 2:3]),
                                 reads=[(uk, "b"), "cw"], writes=[ck])
                            f.op(f.dve, lambda v, c=c, u=u, cvt=cvt: v.scalar_tensor_tensor(out=cvt[:], in0=u[:, 1:513], scalar=cw[:, c, 1:2],
                                                                                         in1=cvt[:], op0=ALU.mult, op1=ALU.add),
                                 reads=[(uk, "b"), (uk, "h"), "cw", ck], writes=[ck])
                            f.op(f.dve, lambda v, c=c, u=u, cvt=cvt: v.scalar_tensor_tensor(out=cvt[:], in0=u[:, 0:512], scalar=cw[:, c, 0:1],
                                                                                         in1=cvt[:], op0=ALU.mult, op1=ALU.add),
                                 reads=[(uk, "b"), (uk, "h"), "cw", ck], writes=[ck])
                        return cvs

                    PRE = 2
                    pend = None
                    for b in range(NB):
                        if b + 1 < NB:
                            ld_h(b + 1)
                        for j in range(PRE if b > 0 else 0, NJ):
                            cvs = up_pair(b, j)
                            if pend is not None:
                                gate_mult(*pend)
                            pend = (j, cvs)
                        gate_mult(*pend)
                        pend = None
                        held = []
                        if b + 1 < NB:
                            for j in range(PRE):
                                held.append((j, up_pair(b + 1, j)))
                        mkeys = [("MT", j) for j in range(NJ)]
                        for tl in range(4):
                            n = b * 4 + tl
                            xs, xk = xt[n % 2], f"xt{n % 2}"
                            if n + 1 < NT:
                                ld_x(n + 1)
                            for half in range(2):
                                pbi = pbank[0] % 6
                                pbank[0] += 1
                                for j in range(NJ):
                                    f.mm(lambda t, j=j, half=half, pbi=pbi: t.matmul(PB[pbi][:], lhsT=MT[:, j, tl * 128:(tl + 1) * 128],
                                                                                   rhs=Wdn[:, j, half * 512:(half + 1) * 512],
                                                                                   start=(j == 0), stop=(j == NJ - 1)),
                                         reads=[("Wdn", j), mkeys[j]], writes=[f"PB{pbi}"], last=(j == NJ - 1))
                                f.op(f.dve, lambda v, half=half, pbi=pbi: v.tensor_tensor(out=xs[:, half * 512:(half + 1) * 512], in0=PB[pbi][:],
                                                                                        in1=xs[:, half * 512:(half + 1) * 512], op=ALU.add),
                                     reads=[], writes=[xk, f"PB{pbi}"])
                            if not final:
                                f.dma(f.sp, f"xsf{n % 2}", Xout[n * 128:(n + 1) * 128, :], xs[:], reads=[xk], writes=[("Xout", n)])
                            else:
                                junk, ss, rs = tmp["junk"], tmp["ss"], tmp["rs"]
                                f.op(f.dve, lambda v: v.scalar_tensor_tensor(out=junk[:], in0=xs[:], scalar=1.0, in1=xs[:], op0=ALU.mult,
                                                                            op1=ALU.mult, accum_out=ss[:, 0:1]),
                                     reads=[xk], writes=["junk", "ss"])
                                f.op(f.act, lambda a: a.activation(out=ss[:, 1:2], in_=ss[:, 0:1], func=AF.Sqrt, bias=EPS, scale=1.0 / D),
                                     reads=["ss"], writes=["ss2"])
                                f.op(f.dve, lambda v: v.reciprocal(rs[:], ss[:, 1:2]), reads=["ss2"], writes=["rs"])
                                f.op(f.dve, lambda v: v.scalar_tensor_tensor(out=xs[:], in0=xs[:], scalar=rs[:, 0:1], in1=G[:], op0=ALU.mult,
                                                                            op1=ALU.mult),
                                     reads=["rs", "G"], writes=[xk])
                                f.dma(f.sp, f"xsf{n % 2}", out[n * 128:(n + 1) * 128, :], xs[:], reads=[xk], writes=[("out", n)])
                        for hj in held[:-1]:
                            gate_mult(*hj)
                        if held:
                            pend = held[-1]
                    f.barrier()

        def pass_B1():
            NSB = S // 2048
            with contextlib.ExitStack() as es:
                Wkv = sbt(es, "Wkv", [128, 8, 3072], BF16)
                Wq = sbt(es, "Wq", [128, 8, 1536], BF16)
                Gkv = sbt(es, "Gkv", [128, D], F32)
                Gq = sbt(es, "Gq", [128, D], F32)
                xt = [sbt(es, f"xt{i}", [128, D], F32) for i in range(2)]
                hb = [[sbt(es, f"hb{w_}_{i}", [128, D], BF16) for i in range(2)] for w_ in range(2)]
                hkvT = sbt(es, "hkvT", [128, 8, 2048], BF16)
                hqTb = [sbt(es, f"hqT{i}", [128, 8, 512], BF16) for i in range(2)]
                st = [sbt(es, f"st{i}", [128, 512], BF16) for i in range(3)]
                st2 = sbt(es, "st2", [128, 8, 2048], BF16)
                vst = [sbt(es, f"vst{i}", [128, 512], BF16) for i in range(2)]
                tmp = mk_norm_tmp(es)
                load_gain(Gkv, "Gkv", 4)
                load_gain(Gq, "Gq", 1)
                load_w(Wq, "Wq", b_w_q, 8, 1536, sem="wlq")
                load_w(Wkv, "Wkv", w_kv, 8, 3072, sem="wlk")
                wkv_keys = [("Wkv", kc) for kc in range(8)]
                wq_keys = [("Wq", kc) for kc in range(8)]
                bank = [0]
                ev = [0]
                sc = [0]
                vc = [0]
                tc = [0]

                def evac(dst_ap, dkey, pbi, src_ap, extra_reads=()):
                    e = f.act if ev[0] % 2 == 0 else f.dve
                    ev[0] += 1
                    if e is f.act:
                        f.op(e, lambda a: a.copy(dst_ap, src_ap), reads=list(extra_reads), writes=[dkey, f"PB{pbi}"])
                    else:
                        f.op(e, lambda v: v.tensor_copy(dst_ap, src_ap), reads=list(extra_reads), writes=[dkey, f"PB{pbi}"])

                def ld_x(n):
                    f.dma(f.sp, f"xl{n % 2}", xt[n % 2][:], X2[n * 128:(n + 1) * 128, :], writes=[f"xt{n % 2}"])

                ld_x(0)
                Gs = ((Gkv, "Gkv"), (Gq, "Gq"))

                def norm_chain(n):
                    xs, xk = xt[n % 2], f"xt{n % 2}"
                    if n + 1 < NT:
                        ld_x(n + 1)
                    rstd_of(es, xs[:], xk, tmp)
                    rs = tmp["rs"]
                    for which, (Gt, gk) in enumerate(Gs):
                        h, hbk = hb[which][n % 2], f"hb{which}_{n % 2}"
                        f.op(f.dve, lambda v, h=h, Gt=Gt: v.scalar_tensor_tensor(out=h[:], in0=xs[:], scalar=rs[:, 0:1], in1=Gt[:],
                                                                              op0=ALU.mult, op1=ALU.mult),
                             reads=[xk, "rs", gk], writes=[hbk])

                def norm_trans(n):
                    b_ = n // 4
                    bl_, tl_ = b_ % 4, n % 4
                    for which in range(2):
                        h, hbk = hb[which][n % 2], f"hb{which}_{n % 2}"
                        pi = tc[0] % 2
                        tc[0] += 1
                        ptb = PTB[pi]
                        for kc in range(8):
                            f.mm(lambda t, kc=kc, h=h, ptb=ptb: t.transpose(ptb[:, kc * 128:(kc + 1) * 128], h[:, kc * 128:(kc + 1) * 128],
                                                                          ident[:]),
                                 reads=[hbk, "ident"], writes=[f"PT{pi}"], last=(kc == 7))
                        if which == 0:
                            c0 = bl_ * 512 + tl_ * 128
                            f.op(f.act, lambda a, c0=c0, ptb=ptb: a.copy(hkvT[:, :, c0:c0 + 128], ptb[:].rearrange("p (k t) -> p k t", k=8)),
                                 writes=[("hkvT", bl_, tl_), f"PT{pi}"])
                        else:
                            hq_ = hqTb[b_ % 2]
                            f.op(f.act, lambda a, ptb=ptb, hq_=hq_: a.copy(hq_[:, :, tl_ * 128:(tl_ + 1) * 128],
                                                                         ptb[:].rearrange("p (k t) -> p k t", k=8)),
                                 writes=[(f"hqT{b_ % 2}", tl_), f"PT{pi}"])

                gcount = [0]

                def maybe_next(b_next):
                    gi = gcount[0]
                    gcount[0] += 1
                    if b_next is None:
                        return
                    if gi % 8 == 1:
                        norm_chain(b_next * 4 + gi // 8)
                    if gi % 8 == 6:
                        norm_trans(b_next * 4 + gi // 8)

                for sbk in range(NSB):
                    for tl in range(4):
                        norm_chain(sbk * 16 + tl)
                        norm_trans(sbk * 16 + tl)
                    for bl in range(4):
                        b = sbk * 4 + bl
                        hqT = hqTb[b % 2]
                        gcount[0] = 0
                        b_next = b + 1 if bl < 3 else None
                        kvkeys = [("hkvT", bl, tl) for tl in range(4)]
                        qkeys = [(f"hqT{b % 2}", tl) for tl in range(4)]
                        for isk in range(2):
                            for p in range(12):
                                g = p // 4
                                pbi = bank[0] % 4
                                bank[0] += 1
                                for kc in range(8):
                                    if isk:
                                        f.mm(lambda t, kc=kc, p=p, pbi=pbi: t.matmul(PB[pbi][:], lhsT=Wkv[:, kc, p * 128:(p + 1) * 128],
                                                                                  rhs=hkvT[:, kc, bl * 512:(bl + 1) * 512],
                                                                                  start=(kc == 0), stop=(kc == 7)),
                                             reads=[wkv_keys[kc]] + kvkeys, writes=[f"PB{pbi}"], last=(kc == 7))
                                    else:
                                        f.mm(lambda t, kc=kc, p=p, pbi=pbi: t.matmul(PB[pbi][:], lhsT=Wq[:, kc, p * 128:(p + 1) * 128],
                                                                                  rhs=hqT[:, kc, :], start=(kc == 0), stop=(kc == 7)),
                                             reads=[wq_keys[kc]] + qkeys, writes=[f"PB{pbi}"], last=(kc == 7))
                                dst = KTB if isk else QTB
                                if g == 0:
                                    s_, sk = st[sc[0] % 3], f"st{sc[0] % 3}"
                                    sc[0] += 1
                                    evac(s_[:], sk, pbi, PB[pbi][:])
                                    f.dma(f.sp, sk, dst[p * 128:(p + 1) * 128, b * 512:(b + 1) * 512], s_[:], reads=[sk], writes=[("qkb", isk, p, b)])
                                elif g == 1:
                                    s_, sk = st[sc[0] % 3], f"st{sc[0] % 3}"
                                    sc[0] += 1
                                    evac(s_[:].rearrange("p (r i) -> p r i", r=4), sk, pbi, PB[pbi][:].rearrange("p (i r) -> p r i", r=4))
                                    f.dma(f.sp, sk, dst[p * 128:(p + 1) * 128, b * 512:(b + 1) * 512], s_[:], reads=[sk], writes=[("qkb", isk, p, b)])
                                else:
                                    q = isk * 4 + (p - 8)
                                    dv = st2[:, q, :].rearrange("p (r i) -> p r i", r=16)[:, :, bl * 32:(bl + 1) * 32]
                                    evac(dv, ("st2", q, bl), pbi, PB[pbi][:].rearrange("p (i r) -> p r i", r=16))
                                    if bl == 3:
                                        f.dma(f.sp, "st2", dst[p * 128:(p + 1) * 128, sbk * 2048:(sbk + 1) * 2048], st2[:, q, :],
                                              reads=[("st2", q, k_) for k_ in range(4)], writes=[("qkb", isk, p, sbk)])
                                maybe_next(b_next)
                        for g in range(2):
                            dil = B_GROUPS[g][1]
                            for r in range(4):
                                if g == 0:
                                    lh = lambda kc, r=r: hkvT[:, kc, bl * 512 + r * 128: bl * 512 + (r + 1) * 128]
                                else:
                                    lh = lambda kc, r=r: hkvT[:, kc, bl * 512 + r: bl * 512 + 512: 4]
                                pbi = bank[0] % 4
                                bank[0] += 1
                                for kc in range(8):
                                    f.mm(lambda t, kc=kc, lh=lh, pbi=pbi, g=g: t.matmul(PB[pbi][:], lhsT=lh(kc),
                                                                                     rhs=Wkv[:, kc, 1536 + g * 512:1536 + (g + 1) * 512],
                                                                                     start=(kc == 0), stop=(kc == 7)),
                                         reads=[wkv_keys[kc]] + kvkeys, writes=[f"PB{pbi}"], last=(kc == 7))
                                v_, vk = vst[vc[0] % 2], f"vst{vc[0] % 2}"
                                vc[0] += 1
                                evac(v_[:], vk, pbi, PB[pbi][:])
                                tau = b * 4 + r
                                f.dma(f.sp, vk, VB[g, tau * 128:(tau + 1) * 128, :], v_[:], reads=[vk], writes=[("vb", g, tau)])
                                maybe_next(b_next)
                    allkv = [("hkvT", bl, tl) for bl in range(4) for tl in range(4)]
                    for r in range(16):
                        pbi = bank[0] % 4
                        bank[0] += 1
                        for kc in range(8):
                            f.mm(lambda t, kc=kc, r=r, pbi=pbi: t.matmul(PB[pbi][:], lhsT=hkvT[:, kc, r:2048:16],
                                                                      rhs=Wkv[:, kc, 1536 + 1024:1536 + 1536],
                                                                      start=(kc == 0), stop=(kc == 7)),
                                 reads=[wkv_keys[kc]] + allkv, writes=[f"PB{pbi}"], last=(kc == 7))
                        v_, vk = vst[vc[0] % 2], f"vst{vc[0] % 2}"
                        vc[0] += 1
                        evac(v_[:], vk, pbi, PB[pbi][:])
                        tau = sbk * 16 + r
                        f.dma(f.sp, vk, VB[2, tau * 128:(tau + 1) * 128, :], v_[:], reads=[vk], writes=[("vb", 2, tau)])
                f.barrier()

        def pass_B2():
            with contextlib.ExitStack() as es:
                Qg = [[sbt(es, f"Qg{s}_{g}", [128, S], BF16) for g in range(3)] for s in range(2)]
                Kg = [[sbt(es, f"Kg{s}_{g}", [128, S], BF16) for g in range(3)] for s in range(2)]
                Vg = [[sbt(es, f"Vg{s}_{g}", [128, NT, 128], BF16) for g in range(3)] for s in range(2)]
                Et = [sbt(es, f"Et{s}", [128, 3, 256], F32) for s in range(2)]
                Nacc = sbt(es, "Nacc", [128, S], F32)
                Pf = [sbt(es, f"Pf{i}", [128, 256], F32) for i in range(4)]
                Pb = [sbt(es, f"Pb{i}", [128, 256], BF16) for i in range(6)]
                Rr = sbt(es, "Rb", [64, S], F32)
                Oh = [sbt(es, f"Oh{i}", [64, S], BF16) for i in range(2)]
                for s in range(2):
                    for g in range(3):
                        f.op(f.pool, lambda gp, s=s, g=g: gp.memset(Vg[s][g][:, :, 64:128], 1.0), writes=[(f"Vg{s}_{g}", "ones")])
                        f.op(f.pool, lambda gp, s=s, g=g: gp.memset(Qg[s][g][64:128, :], 0.0), writes=[(f"Qg{s}_{g}", "z")])
                        f.op(f.pool, lambda gp, s=s, g=g: gp.memset(Kg[s][g][64:128, :], 0.0), writes=[(f"Kg{s}_{g}", "z")])

                def load_head(hh):
                    s = hh % 2
                    for g in range(3):
                        r0 = g * 512 + hh * 64
                        f.dma(f.sp, f"bq{s}", Qg[s][g][0:64, :], QTB[r0:r0 + 64, :], writes=[f"Qg{s}_{g}"])
                        f.dma(f.sp, f"bk{s}", Kg[s][g][0:64, :], KTB[r0:r0 + 64, :], writes=[f"Kg{s}_{g}"])
                        f.dma(f.sp, f"bv{s}", Vg[s][g][:, :, 0:64], VB[g].rearrange("(n p) c -> p n c", p=128)[:, :, hh * 64:(hh + 1) * 64],
                              writes=[(f"Vg{s}_{g}", "v")])
                        f.dma(f.sp, f"be{s}", Et[s][:, g, :], c_etab[g * 8 + hh], writes=[(f"Et{s}", g)])

                load_head(0)
                uc = [0]
                LA = 3
                for hh in range(8):
                    if hh + 1 < 8:
                        load_head(hh + 1)
                    s = hh % 2
                    units = [(g, tau) for g in range(3) for tau in range(NT)]
                    info = {}

                    def emit_qk(u):
                        g, tau = u
                        dil = B_GROUPS[g][1]
                        prev = tau - dil
                        w = 256 if prev >= 0 else 128
                        uid = uc[0]
                        uc[0] += 1
                        sb_ = uid % 4
                        info[u] = (w, uid)
                        Q, K = Qg[s][g], Kg[s][g]
                        f.mm(lambda t: t.matmul(PB[sb_][:, 0:128], lhsT=K[:, tau * 128:(tau + 1) * 128], rhs=Q[:, tau * 128:(tau + 1) * 128],
                                                start=True, stop=True),
                             reads=[f"Qg{s}_{g}", f"Kg{s}_{g}", (f"Qg{s}_{g}", "z"), (f"Kg{s}_{g}", "z")], writes=[f"PB{sb_}"], last=(prev < 0))
                        if prev >= 0:
                            f.mm(lambda t: t.matmul(PB[sb_][:, 128:256], lhsT=K[:, prev * 128:(prev + 1) * 128], rhs=Q[:, tau * 128:(tau + 1) * 128],
                                                    start=True, stop=True),
                                 reads=[f"Qg{s}_{g}", f"Kg{s}_{g}", (f"Qg{s}_{g}", "z"), (f"Kg{s}_{g}", "z")], writes=[f"PB{sb_}"], last=True)
                        pf, pfk = Pf[uid % 4], f"Pf{uid % 4}"
                        f.op(f.act, lambda a: a.activation(out=pf[:, 0:w], in_=PB[sb_][:, 0:w], func=AF.Exp, scale=0.125),
                             writes=[pfk, f"PB{sb_}"])
                        pb, pbk = Pb[uid % 6], f"Pb{uid % 6}"
                        em = f.dve if uid % 3 == 0 else f.pool
                        f.op(em, lambda v: v.tensor_tensor(out=pb[:, 0:w], in0=pf[:, 0:w], in1=Et[s][:, g, 0:w], op=ALU.mult),
                             reads=[pfk, (f"Et{s}", g)], writes=[pbk])

                    def emit_pv(u):
                        g, tau = u
                        dil = B_GROUPS[g][1]
                        prev = tau - dil
                        w, uid = info[u]
                        ob = 4 + (uid % 2)
                        pb, pbk = Pb[uid % 6], f"Pb{uid % 6}"
                        V = Vg[s][g]
                        vks = [(f"Vg{s}_{g}", "v"), (f"Vg{s}_{g}", "ones")]
                        f.mm(lambda t: t.matmul(PB[ob][:, 0:128], lhsT=V[:, tau, :], rhs=pb[:, 0:128], start=True, stop=(prev < 0)),
                             reads=vks + [pbk], writes=[f"PB{ob}"], last=(prev < 0))
                        if prev >= 0:
                            f.mm(lambda t: t.matmul(PB[ob][:, 0:128], lhsT=V[:, prev, :], rhs=pb[:, 128:256], start=False, stop=True),
                                 reads=vks + [pbk], writes=[f"PB{ob}"], last=True)
                        n_, r_ = tau // dil, tau % dil
                        base = n_ * 128 * dil + r_
                        dstv = Nacc[:, base: base + 127 * dil + 1: dil]
                        if g == 0:
                            f.op(f.dve, lambda v: v.tensor_copy(dstv, PB[ob][:, 0:128]), writes=[("Nacc", tau // 16), f"PB{ob}"])
                        else:
                            f.op(f.dve, lambda v: v.tensor_tensor(out=dstv, in0=PB[ob][:, 0:128], in1=dstv, op=ALU.add),
                                 writes=[("Nacc", (n_ * dil) // 16), f"PB{ob}"])

                    for idx in range(len(units) + LA):
                        if idx < len(units):
                            emit_qk(units[idx])
                        if idx - LA >= 0:
                            emit_pv(units[idx - LA])
                    nkeys = [("Nacc", k_) for k_ in range(NT // 16)]
                    f.op(f.act, lambda a: a.activation(out=Rr[:], in_=Nacc[64:128, :], func=AF.Ln), reads=nkeys, writes=["Rb"])
                    f.op(f.act, lambda a: a.activation(out=Rr[:], in_=Rr[:], func=AF.Exp, scale=-1.0), reads=["Rb"], writes=["Rb"])
                    f.op(f.dve, lambda v: v.tensor_tensor(out=Oh[s][:], in0=Nacc[0:64, :], in1=Rr[:], op=ALU.mult),
                         reads=nkeys + ["Rb"], writes=[f"Oh{s}"])
                    f.dma(f.sp, f"bo{s}", OTB[hh * 64:(hh + 1) * 64, :], Oh[s][:], reads=[f"Oh{s}"], writes=[("OTB", hh)])
                f.barrier()

        pass_A1()
        load_w(Wdn0, "Wdn", ffn_w_down[0], NJ, D, sem="wld")
        pass_A2()
        esA.close()
        pass_OF(0, OT, 8, a_w_out, x_in, X1, 2, H1T, X2, False, Wdn_pre=Wdn0)
        esW0.close()
        pass_B1()
        pass_B2()
        pass_OF(1, OTB, 4, b_w_out, X2, X3, 3, H3T, None, True)
        f.barrier()
    return nc


def _consts():
    ident = np.eye(128, dtype=np.float32)
    s = np.arange(128)[:, None]
    t = np.arange(128)[None, :]
    tri = (t >= s).astype(np.float32)
    utri = (s <= t).astype(np.float32)
    ones = np.ones((128, 128), np.float32)
    et = np.zeros((24, 128, 256), np.float64)
    j = np.arange(128)[:, None].astype(np.float64)
    i = np.arange(128)[None, :].astype(np.float64)
    for g, (_, dil) in enumerate(B_GROUPS):
        for hh in range(8):
            k = g * 8 + hh + 1
            slope = np.float64(np.float32(2.0) ** np.float32(-8.0 * k / 24.0))
            sig = slope * dil
            cur = np.where(i >= j, np.exp(-sig * (i - j)), 0.0)
            prv = np.where(i <= j, np.exp(-sig * (i + 128 - j)), 0.0)
            et[g * 8 + hh, :, 0:128] = cur
            et[g * 8 + hh, :, 128:256] = prv
    return {
        "c_ident": ident.astype(ml_dtypes.bfloat16),
        "c_tri": tri.astype(ml_dtypes.bfloat16),
        "c_utri": utri,
        "c_ones": ones,
        "c_etab": et.astype(np.float32),
    }


def _common_inputs(inp):
    f32 = lambda a: np.ascontiguousarray(np.asarray(a, dtype=np.float32))
    gains = np.zeros((7, D), np.float32)
    gains[0:2] = f32(inp["mix_norm_g"])
    gains[2:4] = f32(inp["ffn_norm_g"])
    gains[4] = f32(inp["kv_norm_g"])
    gains[5] = f32(inp["final_norm_g"])
    cw = f32(inp["ffn_conv_w"])
    cb = f32(inp["ffn_conv_b"])
    cwb = np.concatenate([cw, cb[:, None, :]], axis=1)
    cwb = np.ascontiguousarray(cwb.reshape(2, 4, NCH, 128).transpose(0, 3, 2, 1))
    d = {
        "a_w_in": f32(inp["a_w_in"])[0], "a_b_f": f32(inp["a_b_f"]).reshape(1, 16), "a_w_out": f32(inp["a_w_out"])[0],
        "b_w_q": f32(inp["b_w_q"])[0], "b_w_out": f32(inp["b_w_out"])[0], "w_kv": f32(inp["w_kv"]),
        "gains": gains, "ffn_w_up": f32(inp["ffn_w_up"]), "ffn_w_down": f32(inp["ffn_w_down"]), "cwb": cwb,
    }
    d.update(_consts())
    return d


def kernel(**inputs):
    x = np.asarray(inputs["x"], dtype=np.float32)
    B, S, _ = x.shape
    nc = build(S)
    common = _common_inputs(inputs)
    in_maps = [dict(common, x=np.ascontiguousarray(x[b])) for b in range(B)]
    res = run_bass_kernel_spmd(nc, in_maps, core_ids=list(range(B)))
    return np.stack([np.asarray(r["out"], dtype=np.float32) for r in res.results], axis=0)
```
